# Optimizing a Trainium2 kernel written in Bass

```python
import math
import jax, jax.numpy as jnp
from jax import lax
import numpy as np

D_MODEL = 1024
BATCH = 8
SEQ = 2048
DEPTH = 2

N_MIXERS = 2
BRANCH_WIDTH = 2 * D_MODEL
XQ_WIDTH = BRANCH_WIDTH // 4
PRIMARY_WIDTH = BRANCH_WIDTH - XQ_WIDTH
MEM_LEN = 256
X_HEADS = 4
X_HEAD_DIM = XQ_WIDTH // X_HEADS
S5_GROUP_CH = 16
S5_GROUPS = PRIMARY_WIDTH // S5_GROUP_CH
S5_STATE = 64
S5_STEP_MIN = 1e-3
S5_STEP_MAX = 1e-1
MLA_NOPE = 128
MLA_ROPE = 64
MLA_V = 128
MLA_HEADS = PRIMARY_WIDTH // MLA_V
MLA_Q_LORA = D_MODEL // 2
MLA_KV_LORA = D_MODEL // 4
ROPE_THETA = 10000.0
Q_BLOCK = 128
EPS = 1e-6
N_S5 = (DEPTH + 1) // 2
N_MLA = DEPTH // 2
S5_IN_WIDTH = PRIMARY_WIDTH + XQ_WIDTH + BRANCH_WIDTH
MLA_IN_WIDTH = MLA_Q_LORA + MLA_KV_LORA + MLA_ROPE + XQ_WIDTH + BRANCH_WIDTH

kernel_name = "hybrid_s5_mla_memory_block"


def rms_norm(x, g):
    xf = x.astype(jnp.float32)
    y = xf * lax.rsqrt(jnp.mean(xf * xf, axis=-1, keepdims=True) + EPS)
    return (y * g.astype(jnp.float32)).astype(x.dtype)


def rotary_tables(positions):
    half = MLA_ROPE // 2
    inv_freq = ROPE_THETA ** (-jnp.arange(half, dtype=jnp.float32) / half)
    ang = positions.astype(jnp.float32)[:, :, None, None] * inv_freq
    return jnp.cos(ang), jnp.sin(ang)


def rotary(x, cos, sin):
    x1, x2 = jnp.split(x.astype(jnp.float32), 2, axis=-1)
    return jnp.concatenate([x1 * cos - x2 * sin, x1 * sin + x2 * cos], axis=-1).astype(x.dtype)


def _ssm_combine(left, right):
    a_l, b_l = left
    a_r, b_r = right
    return a_l * a_r, a_r * b_l + b_r


def s5_mix(u, lam_re, lam_im, log_step, b_re, b_im, c_re, c_im, d):
    bsz, seq, _ = u.shape
    f32 = jnp.float32
    uf = u.astype(f32).reshape(bsz, seq, S5_GROUPS, S5_GROUP_CH)
    lam = lax.complex(lam_re.astype(f32), lam_im.astype(f32))
    step = jnp.exp(log_step.astype(f32))[:, None]
    a_bar = jnp.exp(lam * step)
    b_mat = lax.complex(b_re.astype(f32), b_im.astype(f32))
    c_mat = lax.complex(c_re.astype(f32), c_im.astype(f32))
    b_bar = ((a_bar - 1.0) / lam)[..., None] * b_mat
    bu = jnp.einsum('blgc,gpc->blgp', uf.astype(jnp.complex64), b_bar)
    a_seq = jnp.broadcast_to(a_bar, (1, seq) + a_bar.shape)
    _, state = lax.associative_scan(_ssm_combine, (a_seq, bu), axis=1)
    y = jnp.einsum('blgp,gcp->blgc', state, c_mat).real + d.astype(f32).reshape(S5_GROUPS, S5_GROUP_CH) * uf
    return y.reshape(bsz, seq, PRIMARY_WIDTH).astype(u.dtype)


def causal_block_attention(q, k, v, scale):
    bsz, seq, heads, dk = q.shape
    dv = v.shape[-1]
    n_blocks = seq // Q_BLOCK
    q_blocks = q.reshape(bsz, n_blocks, Q_BLOCK, heads, dk).transpose(1, 0, 2, 3, 4)
    k_pos = jnp.arange(seq)

    def one_block(args):
        q_blk, blk = args
        s = jnp.einsum('bqhd,bkhd->bhqk', q_blk, k).astype(jnp.float32) * scale
        q_pos = blk * Q_BLOCK + jnp.arange(Q_BLOCK)
        s = jnp.where(k_pos[None, :] <= q_pos[:, None], s, jnp.finfo(jnp.float32).min)
        p = jax.nn.softmax(s, axis=-1).astype(v.dtype)
        return jnp.einsum('bhqk,bkhd->bqhd', p, v)

    out = lax.map(one_block, (q_blocks, jnp.arange(n_blocks)))
    return out.transpose(1, 0, 2, 3, 4).reshape(bsz, seq, heads, dv)


def memory_attention(xq, mem, mem_norm, w_mem_kv, xq_norm, xk_norm):
    bsz, seq, _ = xq.shape
    kv = rms_norm(mem, mem_norm) @ w_mem_kv
    k, v = jnp.split(kv, 2, axis=-1)
    k = rms_norm(k.reshape(bsz, -1, X_HEADS, X_HEAD_DIM), xk_norm)
    v = v.reshape(bsz, -1, X_HEADS, X_HEAD_DIM)
    q = rms_norm(xq.reshape(bsz, seq, X_HEADS, X_HEAD_DIM), xq_norm)
    s = jnp.einsum('blhd,bmhd->bhlm', q, k).astype(jnp.float32) * (X_HEAD_DIM ** -0.5)
    p = jax.nn.softmax(s, axis=-1).astype(v.dtype)
    return jnp.einsum('bhlm,bmhd->blhd', p, v).reshape(bsz, seq, XQ_WIDTH)


def merge_branches(x, mixer_out, xq, gate, mem, w_out, mem_norm, w_mem_kv, xq_norm, xk_norm):
    mem_out = memory_attention(xq, mem, mem_norm, w_mem_kv, xq_norm, xk_norm)
    o = jnp.concatenate([mixer_out, mem_out], axis=-1) * jax.nn.silu(gate)
    return x + o @ w_out


def s5_layer(x, mem, ln, w_in, lam_re, lam_im, log_step, b_re, b_im, c_re, c_im, d, w_glu,
             w_out, mem_norm, w_mem_kv, xq_norm, xk_norm):
    proj = rms_norm(x, ln) @ w_in
    u, xq, gate = jnp.split(proj, [PRIMARY_WIDTH, PRIMARY_WIDTH + XQ_WIDTH], axis=-1)
    y = s5_mix(u, lam_re, lam_im, log_step, b_re, b_im, c_re, c_im, d)
    y_a, y_b = jnp.split(jax.nn.gelu(y) @ w_glu, 2, axis=-1)
    y = y_a * jax.nn.sigmoid(y_b)
    return merge_branches(x, y, xq, gate, mem, w_out, mem_norm, w_mem_kv, xq_norm, xk_norm)


def mla_layer(x, mem, cos, sin, ln, w_in, q_lora_norm, kv_lora_norm, w_uq, w_ukv,
              q_nope_norm, k_nope_norm, q_rope_norm, k_rope_norm,
              w_out, mem_norm, w_mem_kv, xq_norm, xk_norm):
    bsz, seq, _ = x.shape
    proj = rms_norm(x, ln) @ w_in
    o1 = MLA_Q_LORA
    o2 = o1 + MLA_KV_LORA
    o3 = o2 + MLA_ROPE
    o4 = o3 + XQ_WIDTH
    c_q, c_kv, k_rope, xq, gate = jnp.split(proj, [o1, o2, o3, o4], axis=-1)
    q = (rms_norm(c_q, q_lora_norm) @ w_uq).reshape(bsz, seq, MLA_HEADS, MLA_NOPE + MLA_ROPE)
    kv = (rms_norm(c_kv, kv_lora_norm) @ w_ukv).reshape(bsz, seq, MLA_HEADS, MLA_NOPE + MLA_V)
    q_nope, q_rope = q[..., :MLA_NOPE], q[..., MLA_NOPE:]
    k_nope, v = kv[..., :MLA_NOPE], kv[..., MLA_NOPE:]
    q_rope = rotary(rms_norm(q_rope, q_rope_norm), cos, sin)
    k_rope = rotary(rms_norm(k_rope.reshape(bsz, seq, 1, MLA_ROPE), k_rope_norm), cos, sin)
    q_full = jnp.concatenate([rms_norm(q_nope, q_nope_norm), q_rope], axis=-1)
    k_full = jnp.concatenate([rms_norm(k_nope, k_nope_norm),
                              jnp.broadcast_to(k_rope, (bsz, seq, MLA_HEADS, MLA_ROPE))], axis=-1)
    attn = causal_block_attention(q_full, k_full, v, (MLA_NOPE + MLA_ROPE) ** -0.5)
    attn = attn.reshape(bsz, seq, PRIMARY_WIDTH)
    return merge_branches(x, attn, xq, gate, mem, w_out, mem_norm, w_mem_kv, xq_norm, xk_norm)


def setup_inputs(seed: int = 0) -> dict:
    key = jax.random.key(seed)
    k = jax.random.split(key, 32)
    f32 = jnp.float32

    def w(kk, shape, fan_in):
        return jax.random.normal(kk, shape, f32) * (fan_in ** -0.5)

    def gain(kk, shape):
        return 1.0 + 0.02 * jax.random.normal(kk, shape, f32)

    x = jax.random.normal(k[0], (BATCH, SEQ, D_MODEL), f32)
    mem = jax.random.normal(k[1], (BATCH, MEM_LEN, D_MODEL), f32)
    offsets = jax.random.randint(k[2], (BATCH, 1), 0, 4096, dtype=jnp.int32)
    positions = offsets + jnp.arange(SEQ, dtype=jnp.int32)[None, :]

    lam_im_base = math.pi * jnp.arange(S5_STATE, dtype=f32)
    return {
        "x": x,
        "mem": mem,
        "positions": positions,
        "ln_gain": gain(k[3], (DEPTH, D_MODEL)),
        "w_out": w(k[4], (DEPTH, BRANCH_WIDTH, D_MODEL), BRANCH_WIDTH),
        "mem_norm": gain(k[5], (DEPTH, D_MODEL)),
        "w_mem_kv": w(k[6], (DEPTH, D_MODEL, 2 * XQ_WIDTH), D_MODEL),
        "xq_norm": gain(k[7], (DEPTH, X_HEAD_DIM)),
        "xk_norm": gain(k[8], (DEPTH, X_HEAD_DIM)),
        "s5_w_in": w(k[9], (N_S5, D_MODEL, S5_IN_WIDTH), D_MODEL),
        "s5_lambda_re": -0.5 + 0.01 * jax.random.normal(k[10], (N_S5, S5_GROUPS, S5_STATE), f32),
        "s5_lambda_im": lam_im_base + 0.01 * jax.random.normal(k[11], (N_S5, S5_GROUPS, S5_STATE), f32),
        "s5_log_step": jax.random.uniform(k[12], (N_S5, S5_GROUPS), f32,
                                          math.log(S5_STEP_MIN), math.log(S5_STEP_MAX)),
        "s5_b_re": w(k[13], (N_S5, S5_GROUPS, S5_STATE, S5_GROUP_CH), 2 * S5_GROUP_CH),
        "s5_b_im": w(k[14], (N_S5, S5_GROUPS, S5_STATE, S5_GROUP_CH), 2 * S5_GROUP_CH),
        "s5_c_re": w(k[15], (N_S5, S5_GROUPS, S5_GROUP_CH, S5_STATE), S5_STATE),
        "s5_c_im": w(k[16], (N_S5, S5_GROUPS, S5_GROUP_CH, S5_STATE), S5_STATE),
        "s5_d": jax.random.normal(k[17], (N_S5, PRIMARY_WIDTH), f32),
        "s5_w_glu": w(k[18], (N_S5, PRIMARY_WIDTH, 2 * PRIMARY_WIDTH), PRIMARY_WIDTH),
        "mla_w_in": w(k[19], (N_MLA, D_MODEL, MLA_IN_WIDTH), D_MODEL),
        "mla_q_lora_norm": gain(k[20], (N_MLA, MLA_Q_LORA)),
        "mla_kv_lora_norm": gain(k[21], (N_MLA, MLA_KV_LORA)),
        "mla_w_uq": w(k[22], (N_MLA, MLA_Q_LORA, MLA_HEADS * (MLA_NOPE + MLA_ROPE)), MLA_Q_LORA),
        "mla_w_ukv": w(k[23], (N_MLA, MLA_KV_LORA, MLA_HEADS * (MLA_NOPE + MLA_V)), MLA_KV_LORA),
        "mla_q_nope_norm": gain(k[24], (N_MLA, MLA_NOPE)),
        "mla_k_nope_norm": gain(k[25], (N_MLA, MLA_NOPE)),
        "mla_q_rope_norm": gain(k[26], (N_MLA, MLA_ROPE)),
        "mla_k_rope_norm": gain(k[27], (N_MLA, MLA_ROPE)),
    }


def reference(x, mem, positions, ln_gain, w_out, mem_norm, w_mem_kv, xq_norm, xk_norm,
              s5_w_in, s5_lambda_re, s5_lambda_im, s5_log_step, s5_b_re, s5_b_im, s5_c_re, s5_c_im,
              s5_d, s5_w_glu, mla_w_in, mla_q_lora_norm, mla_kv_lora_norm, mla_w_uq, mla_w_ukv,
              mla_q_nope_norm, mla_k_nope_norm, mla_q_rope_norm, mla_k_rope_norm):
    cos, sin = rotary_tables(positions)
    for i in range(DEPTH):
        j = i // N_MIXERS
        if i % N_MIXERS == 0:
            x = s5_layer(x, mem, ln_gain[i], s5_w_in[j], s5_lambda_re[j], s5_lambda_im[j], s5_log_step[j],
                         s5_b_re[j], s5_b_im[j], s5_c_re[j], s5_c_im[j], s5_d[j], s5_w_glu[j],
                         w_out[i], mem_norm[i], w_mem_kv[i], xq_norm[i], xk_norm[i])
        else:
            x = mla_layer(x, mem, cos, sin, ln_gain[i], mla_w_in[j], mla_q_lora_norm[j], mla_kv_lora_norm[j],
                          mla_w_uq[j], mla_w_ukv[j], mla_q_nope_norm[j], mla_k_nope_norm[j],
                          mla_q_rope_norm[j], mla_k_rope_norm[j],
                          w_out[i], mem_norm[i], w_mem_kv[i], xq_norm[i], xk_norm[i])
    return x
```

```python
from concourse.bass_utils import run_bass_kernel_spmd
import numpy as np
from contextlib import ExitStack
import concourse.bass as bass
import concourse.mybir as mybir

F32 = mybir.dt.float32
BF16 = mybir.dt.bfloat16
I32 = mybir.dt.int32
ALU = mybir.AluOpType
AF = mybir.ActivationFunctionType
AX = mybir.AxisListType

ENGS = ("pe", "act", "dve", "pool", "sp")


_DTS = {}


def _dsize(dt):
    k = str(dt)
    v = _DTS.get(k)
    if v is None:
        v = 2 if ("16" in k) else (1 if "8" in k else (8 if "64" in k else 4))
        _DTS[k] = v
    return v


def _box(ap):
    t = ap.tensor
    sp = str(ap.space)
    if "DRAM" in sp.upper() or "HBM" in sp.upper():
        return None
    a = ap.ap
    pstride = a[0][0]
    off = int(ap.offset)
    if pstride == 0:
        p0, f0 = 0, off
        npart = 1
    else:
        p0, f0 = off // pstride, off % pstride
        npart = a[0][1]
    ext = 0
    for st, cnt in a[1:]:
        ext += abs(st) * (cnt - 1)
    ds = _dsize(ap.dtype)
    return (t.name, p0, p0 + npart, f0 * ds, (f0 + ext + 1) * ds)


def _ov(a, b):
    return a[0] == b[0] and a[1] < b[2] and b[1] < a[2] and a[3] < b[4] and b[3] < a[4]


def _cov(a, b):
    return a[0] == b[0] and a[1] <= b[1] and a[2] >= b[2] and a[3] <= b[3] and a[4] >= b[4]


class Op:
    __slots__ = ("eng", "fn", "deps", "idx", "tick", "signal", "waits", "dma", "dsem", "dval", "vc")

    def __init__(self, eng, fn):
        self.eng = eng
        self.fn = fn
        self.deps = set()
        self.tick = 0
        self.signal = False
        self.waits = []
        self.dma = False
        self.dsem = None
        self.dval = 0
        self.vc = None


class Sched:
    def __init__(self, nc, n_dma_sems=8):
        self.nc = nc
        self.ops = []
        self.writes = {}
        self.reads = {}
        self.n_dma_sems = n_dma_sems
        self.dma_count = {}
        self.es = ExitStack()

    def sb(self, name, shape, dtype):
        return self.es.enter_context(self.nc.sbuf_tensor(name, list(shape), dtype))

    def ps(self, name, shape, dtype=F32):
        return self.es.enter_context(self.nc.psum_tensor(name, list(shape), dtype))

    def op(self, eng, fn, reads=(), writes=(), dma=False):
        o = Op(eng, fn)
        o.idx = len(self.ops)
        o.dma = dma
        for ap in reads:
            b = _box(ap)
            if b is None:
                continue
            for (wb, wi) in self.writes.get(b[0], ()):
                if _ov(wb, b):
                    o.deps.add(wi)
        for ap in writes:
            b = _box(ap)
            if b is None:
                continue
            wl = self.writes.get(b[0], [])
            rl = self.reads.get(b[0], [])
            for (wb, wi) in wl:
                if _ov(wb, b):
                    o.deps.add(wi)
            for (rb, ri) in rl:
                if _ov(rb, b):
                    o.deps.add(ri)
            self.writes[b[0]] = [(wb, wi) for (wb, wi) in wl if not _cov(b, wb)]
            self.reads[b[0]] = [(rb, ri) for (rb, ri) in rl if not _cov(b, rb)]
        for ap in reads:
            b = _box(ap)
            if b is None:
                continue
            rl = self.reads.setdefault(b[0], [])
            if not dma:
                rl[:] = [(rb, ri) for (rb, ri) in rl
                         if ri == o.idx or not (_cov(b, rb) and self.ops[ri].eng == eng and not self.ops[ri].dma)]
            rl.append((b, o.idx))
        for ap in writes:
            b = _box(ap)
            if b is None:
                continue
            self.writes.setdefault(b[0], []).append((b, o.idx))
        o.deps.discard(o.idx)
        self.ops.append(o)
        return o

    def finalize(self):
        ops = self.ops
        need = {}
        for o in ops:
            nd = set()
            best = {}
            for d in o.deps:
                p = ops[d]
                if p.dma:
                    nd.add(d)
                    continue
                if p.eng != o.eng:
                    ok = True
                else:
                    ok = (o.eng in ("act", "dve", "pool") and not o.dma) or (o.dma and not p.dma)
                if ok and best.get(p.eng, -1) < d:
                    best[p.eng] = d
            nd.update(best.values())
            need[o.idx] = nd
            for d in nd:
                ops[d].signal = True
        cnt = {e: 0 for e in ENGS}
        for o in ops:
            if o.dma:
                continue
            if o.signal:
                cnt[o.eng] += 1
                o.tick = cnt[o.eng]
        dcount = {}
        self.dma_sem_use = {}
        for o in ops:
            if o.dma:
                k = dcount.get(o.eng, 0)
                dcount[o.eng] = k + 1
                slot = (o.eng, k % self.n_dma_sems)
                u = self.dma_sem_use.get(slot, 0) + 1
                self.dma_sem_use[slot] = u
                o.dsem = slot
                o.dval = 16 * u
        known = {e: {} for e in ENGS}
        for o in ops:
            kn = known[o.eng]
            waits = []
            if o.dma and o.dval > 16:
                key = o.dsem
                if kn.get(key, 0) < o.dval - 16:
                    waits.append((key, o.dval - 16))
                    kn[key] = o.dval - 16
            for d in sorted(need[o.idx]):
                p = ops[d]
                if p.dma:
                    key, val = p.dsem, p.dval
                else:
                    key, val = p.eng, p.tick
                if kn.get(key, 0) >= val:
                    continue
                waits.append((key, val))
                kn[key] = val
                if not p.dma and p.vc is not None:
                    for k2, v2 in p.vc.items():
                        if kn.get(k2, 0) < v2:
                            kn[k2] = v2
            wm = {}
            for k_, v_ in waits:
                if wm.get(k_, 0) < v_:
                    wm[k_] = v_
            o.waits = list(wm.items())
            if o.signal and not o.dma:
                o.vc = dict(kn)
                kn[o.eng] = max(kn.get(o.eng, 0), 0)

    def emit(self, final_waits_engine="sp"):
        nc = self.nc
        ops = self.ops
        sems = {}
        es = self.es
        for e in ENGS:
            sems[e] = es.enter_context(nc.semaphore("s_" + e))
        for slot in self.dma_sem_use:
            sems[slot] = es.enter_context(nc.semaphore("d_%s_%d" % slot))
        per = {e: [o for o in ops if o.eng == e] for e in ENGS}
        finals = []
        for slot, u in self.dma_sem_use.items():
            finals.append((slot, 16 * u))
        cnt = {e: max([o.tick for o in per[e] if not o.dma] + [0]) for e in ENGS}
        for e in ENGS:
            if cnt[e] > 0:
                finals.append((e, cnt[e]))

        def run(engname, eng):
            for o in per[engname]:
                for key, val in o.waits:
                    eng.wait_ge(sems[key], val)
                ins = o.fn(eng)
                if o.dma:
                    ins.then_inc(sems[o.dsem], 16)
                elif o.signal:
                    ins.then_inc(sems[o.eng], 1)
            if engname == final_waits_engine:
                for key, val in finals:
                    eng.wait_ge(sems[key], val)

        with nc.Block() as block:
            @block.tensor
            def _(e):
                run("pe", e)

            @block.scalar
            def _(e):
                run("act", e)

            @block.vector
            def _(e):
                run("dve", e)

            @block.gpsimd
            def _(e):
                run("pool", e)

            @block.sync
            def _(e):
                run("sp", e)

    def dma(self, out, in_, eng="sp", **kw):
        return self.op(eng, lambda e: e.dma_start(out=out, in_=in_, **kw), reads=[in_], writes=[out], dma=True)

    def mm(self, out, lhsT, rhs, start=True, stop=True, **kw):
        return self.op("pe", lambda e: e.matmul(out, lhsT, rhs, start=start, stop=stop, **kw),
                       reads=[lhsT, rhs] + ([] if start else [out]), writes=[out])

    def tr(self, out, in_, ident, **kw):
        return self.op("pe", lambda e: e.transpose(out, in_, ident, **kw), reads=[in_, ident], writes=[out])

    def act(self, out, in_, func, bias=None, scale=None, accum_out=None, eng="act"):
        kw = {}
        rd = [in_]
        wr = [out]
        if bias is not None:
            kw["bias"] = bias
            if not isinstance(bias, (int, float)):
                rd.append(bias)
        if scale is not None:
            kw["scale"] = scale
            if not isinstance(scale, (int, float)):
                rd.append(scale)
        if accum_out is not None:
            kw["accum_out"] = accum_out
            wr.append(accum_out)
        return self.op(eng, lambda e: e.activation(out, in_, func, **kw), reads=rd, writes=wr)

    def tt(self, out, in0, in1, op, eng="dve"):
        return self.op(eng, lambda e: e.tensor_tensor(out, in0, in1, op), reads=[in0, in1], writes=[out])

    def ts(self, out, in0, s1, s2, op0, op1=None, eng="dve", accum_out=None):
        rd = [in0]
        for s in (s1, s2):
            if s is not None and not isinstance(s, (int, float)):
                rd.append(s)
        wr = [out]
        kw = {}
        if accum_out is not None:
            wr.append(accum_out)
            kw["accum_out"] = accum_out
        if op1 is None:
            return self.op(eng, lambda e: e.tensor_scalar(out, in0, s1, None, op0, **kw), reads=rd, writes=wr)
        return self.op(eng, lambda e: e.tensor_scalar(out, in0, s1, s2, op0, op1, **kw), reads=rd, writes=wr)

    def stt(self, out, in0, scalar, in1, op0, op1, eng="dve"):
        rd = [in0, in1]
        if not isinstance(scalar, (int, float)):
            rd.append(scalar)
        return self.op(eng, lambda e: e.scalar_tensor_tensor(out, in0, scalar, in1, op0, op1), reads=rd, writes=[out])

    def copy(self, out, in_, eng="dve"):
        if eng == "act":
            return self.op(eng, lambda e: e.copy(out, in_), reads=[in_], writes=[out])
        return self.op(eng, lambda e: e.tensor_copy(out, in_), reads=[in_], writes=[out])

    def memset(self, out, val, eng="dve"):
        return self.op(eng, lambda e: e.memset(out, val), reads=[], writes=[out])

    def scan(self, out, d0, d1, initial, op0, op1, eng="dve"):
        rd = [d0, d1]
        if not isinstance(initial, (int, float)):
            rd.append(initial)
        return self.op(eng, lambda e: e.tensor_tensor_scan(out, d0, d1, initial, op0, op1), reads=rd, writes=[out])

    def recip(self, out, in_, eng="dve"):
        return self.op(eng, lambda e: e.reciprocal(out, in_), reads=[in_], writes=[out])

    def reduce(self, out, in_, op, axis=AX.X, eng="dve"):
        return self.op(eng, lambda e: e.tensor_reduce(out, in_, axis, op), reads=[in_], writes=[out])


D_MODEL = 1024
SEQ = 2048
MEM_LEN = 256
EPS = 1e-6
TWO_PI = 6.283185307179586
AP_ = bass.AP


def mk(base, dims):
    return AP_(base.tensor, base.offset, [list(base.ap[0])] + [list(d) for d in dims])


class Arena:
    def __init__(self, S, nbytes):
        self.t = S.sb("arena", [128, nbytes // 4], F32)
        self.free = [[0, nbytes, 0]]
        self.live = {}
        self.k = 0
        self.clock = 1

    def _view(self, o, nb, shape, dtype, n):
        v = self.t[:, o // 4:(o + nb) // 4]
        if dtype != F32:
            v = v.bitcast(dtype)
        v = v[:, 0:n]
        if len(shape) == 2:
            v = v.rearrange("p (a b) -> p a b", a=shape[0])
        elif len(shape) == 3:
            v = v.rearrange("p (a b c) -> p a b c", a=shape[0], b=shape[1])
        elif len(shape) == 4:
            v = v.rearrange("p (a b c d) -> p a b c d", a=shape[0], b=shape[1], c=shape[2])
        return v

    def alloc(self, shape, dtype, persistent=False):
        n = 1
        for d in shape:
            n *= d
        nb = n * _dsize(dtype)
        nb = (nb + 63) // 64 * 64
        cands = [b for b in self.free if b[1] >= nb]
        if not cands:
            raise RuntimeError("arena OOM need %d free %s" % (nb, self.free))
        if persistent or nb >= 16384:
            blk = min(cands, key=lambda b: b[0])
        else:
            blk = min(cands, key=lambda b: (b[2], b[0]))
        o = blk[0]
        if blk[1] == nb:
            self.free.remove(blk)
        else:
            blk[0] += nb
            blk[1] -= nb
        self.k += 1
        self.live[self.k] = (o, nb)
        return self._view(o, nb, shape, dtype, n), self.k

    def release(self, tok):
        o, nb = self.live.pop(tok)
        self.clock += 1
        self.free.append([o, nb, self.clock])
        self.free.sort()
        m = []
        for b in self.free:
            if m and m[-1][0] + m[-1][1] == b[0]:
                m[-1][1] += b[1]
                m[-1][2] = max(m[-1][2], b[2])
            else:
                m.append(b)
        self.free = m


class Ctx:
    pass


def build(layers=(0, 1), dbg=()):
    nc = bass.Bass("TRN2", target_bir_lowering=False)
    S = Sched(nc, n_dma_sems=8)
    Dm = {}

    def din(name, shape, dt=F32):
        Dm[name] = nc.dram_tensor(name, list(shape), dt, kind="ExternalInput").ap()
        return Dm[name]

    x_d = din("x", [SEQ, D_MODEL])
    mem_d = din("mem", [MEM_LEN, D_MODEL])
    pos_d = din("pos", [1, SEQ], I32)
    out_d = nc.dram_tensor("out", [SEQ, D_MODEL], F32, kind="ExternalOutput").ap()
    dbg_d = {}

    A = Arena(S, 212000 // 64 * 64)
    PS = S.ps("psum", [128, 4096], F32)

    def bank(b, n=512, off=0):
        return PS[:, 512 * b + off:512 * b + off + n]

    def bank_bf(b):
        return PS[:, 512 * b:512 * (b + 1)].bitcast(BF16)

    C = Ctx()
    C.dbg = dbg
    C.dbg_d = dbg_d

    def dbg_out(name, ap_sb, shape):
        d = nc.dram_tensor("dbg_" + name, list(shape), F32, kind="ExternalOutput").ap()
        dbg_d[name] = d
        S.dma(d, ap_sb)
    C.dbg_out = dbg_out
    C.nc, C.S, C.A, C.PS, C.bank, C.bank_bf, C.din, C.D = nc, S, A, PS, bank, bank_bf, din, Dm

    ident_d = din("c_ident", [128, 128])
    idf, t_idf = A.alloc([128], F32)
    S.dma(idf, ident_d)
    C.ident, _ = A.alloc([128], BF16)
    S.copy(C.ident, idf)
    C.identf = idf
    C.ones, _ = A.alloc([128], BF16)
    S.memset(C.ones, 1.0)

    class WPool:
        def __init__(self, n_st, n_bf, elems):
            self.bf = [A.alloc([elems], BF16) for _ in range(n_bf)]
            self.c = 0

        def free(self):
            for _, k in self.bf:
                A.release(k)

        def load(self, W, kt_n, c0, ncols, dst=None):
            Wv = W.rearrange("(k p) c -> p k c", p=128)
            if dst is None:
                wb, _ = self.bf[self.c % len(self.bf)]
                self.c += 1
                dv = wb[:, 0:kt_n * ncols].rearrange("p (k c) -> p k c", k=kt_n)
            else:
                dv = dst
            h = max(1, kt_n // 2)
            S.dma(dv[:, 0:h, :], Wv[:, 0:h, c0:c0 + ncols], eng="pool")
            if h < kt_n:
                S.dma(dv[:, h:, :], Wv[:, h:, c0:c0 + ncols], eng="pool")
            return dv
    C.WPool = WPool

    def gain_row(g_d, l):
        g, k = A.alloc([1024], F32)
        S.dma(g, g_d[l:l + 1, :].partition_broadcast(128))
        return g, k
    C.gain_row = gain_row

    def norm_T(src_d, ntt, dstT, gbc):
        for tt in range(ntt):
            xt, t1 = A.alloc([1024], F32)
            S.dma(xt, src_d[tt * 128:(tt + 1) * 128, :])
            norm_T_tile(xt, tt, dstT, gbc, last=(tt == ntt - 1))
            A.release(t1)

    pend_evac = []

    def norm_T_tile(xt, tt, dstT, gbc, last=True):
        junk, t2 = A.alloc([1024], BF16)
        ss, t3 = A.alloc([2], F32)
        S.memset(ss, 0.0)
        S.act(junk, xt, AF.Square, accum_out=ss[:, 0:1])
        C.rstd(ss[:, 0:1], ss[:, 0:1], 1.0 / D_MODEL, tmp=ss[:, 1:2])
        xn, t4 = A.alloc([1024], BF16)
        S.stt(xn, xt, ss[:, 0:1], gbc, ALU.mult, ALU.mult)
        pb = C.bank_bf(C.trb[0] % 2)
        C.trb[0] += 1
        for k in range(8):
            S.tr(pb[:, k * 128:(k + 1) * 128], xn[:, k * 128:(k + 1) * 128], C.ident)
        while pend_evac:
            pend_evac.pop(0)()
        pend_evac.append(lambda: S.copy(dstT[:, :, tt * 128:(tt + 1) * 128], pb.rearrange("p (k c) -> p k c", k=8), eng="dve"))
        if last:
            while pend_evac:
                pend_evac.pop(0)()
        for t in (t2, t3, t4):
            A.release(t)
    C.trb = [0]
    C.eps_col, _ = A.alloc([1], F32)
    S.memset(C.eps_col, EPS)
    C.one_col, _ = A.alloc([1], F32)
    S.memset(C.one_col, 1.0)

    def rstd(out, in_, inv_n, tmp=None):
        npart = out.shape[0]
        t = out if tmp is None else tmp
        S.act(t, in_, AF.Ln, bias=C.eps_col[0:npart], scale=inv_n)
        S.act(out, t, AF.Exp, scale=-0.5)
    C.rstd = rstd

    def sigmoid(out, in_):
        npart = out.shape[0]
        S.act(out, in_, AF.Exp, scale=-1.0)
        S.act(out, out, AF.Ln, bias=C.one_col[0:npart], scale=1.0)
        S.act(out, out, AF.Exp, scale=-1.0)
    C.sigmoid = sigmoid

    def silu_gate(dst, val, pg):
        sg, k1 = A.alloc([512], F32)
        sigmoid(sg, pg)
        S.tt(sg, sg, val, ALU.mult)
        S.tt(dst, sg, pg, ALU.mult)
        A.release(k1)
    C.silu_gate = silu_gate
    C.norm_T, C.norm_T_tile = norm_T, norm_T_tile

    C.lng_d = din("ln_gain", [2, 1024])
    C.mng_d = din("mem_norm", [2, 1024])
    C.mem_d = mem_d
    xqn_d = din("p_xq_norm", [128, 2])
    xkn_d = din("p_xk_norm", [128, 2])
    C.xqn, _ = A.alloc([2], F32, persistent=True)
    C.xkn, _ = A.alloc([2], F32, persistent=True)
    S.dma(C.xqn, xqn_d)
    S.dma(C.xkn, xkn_d)
    wout_d = din("w_out", [2, 2048, 1024])
    wmem_d = din("w_mem_kv", [2, 1024, 1024])
    C.wout_d, C.wmem_d = wout_d, wmem_d


    def fnorm(src_ps, n, gain_col, dst, psb, ndim=128):
        sq, t1 = A.alloc([512], BF16)
        S.act(sq[0:ndim, 0:n], src_ps, AF.Square)
        S.mm(psb[0:ndim, 0:n], C.ones[0:ndim, 0:ndim], sq[0:ndim, 0:n])
        rs, t2 = A.alloc([512], F32)
        C.rstd(rs[0:ndim, 0:n], psb[0:ndim, 0:n], 1.0 / ndim)
        S.stt(dst, src_ps, gain_col, rs[0:ndim, 0:n], ALU.mult, ALU.mult)
        A.release(t1)
        A.release(t2)
    C.fnorm = fnorm

    def mem_kv(l, P, wk_pre=None, wv_pre=None):
        KmT, tk = A.alloc([4, MEM_LEN], BF16)
        Vm, tv = A.alloc([2, 512], BF16)
        memnT, k_mn = A.alloc([8, MEM_LEN], BF16)
        gm, k_gm = C.gain_row(C.mng_d, l)
        norm_T(C.mem_d, 2, memnT, gm)
        A.release(k_gm)
        wk = wk_pre if wk_pre is not None else P.load(wmem_d[l], 8, 0, 512)
        for h in range(4):
            pk = bank(h % 2, 256)
            for k in range(8):
                S.mm(pk, wk[:, k, h * 128:(h + 1) * 128], memnT[:, k, :], start=(k == 0), stop=(k == 7))
            fnorm(pk, 256, C.xkn[:, l:l + 1], KmT[:, h, :], bank(2 + h % 2, 256))
        wv = wv_pre if wv_pre is not None else P.load(wmem_d[l], 8, 512, 512)
        for kt in range(2):
            pv = bank(4 + kt)
            for k in range(8):
                S.mm(pv, memnT[:, k, kt * 128:(kt + 1) * 128], wv[:, k, :], start=(k == 0), stop=(k == 7))
            S.copy(Vm[:, kt, :], pv, eng="act")
        A.release(k_mn)
        return KmT, Vm, tk, tv
    C.mem_kv = mem_kv

    def mem_attn_head(l, h, xq_ps_fn, hT, gate_fn, oT, KmT, Vm):
        pass

    if 0 in layers:
        layer_s5(C, x_d)
    else:
        C.x1, _ = A.alloc([16, 1024], F32)
        for tt in range(16):
            S.dma(C.x1[:, tt, :], x_d[tt * 128:(tt + 1) * 128, :])
    if 1 in layers:
        layer_mla(C, pos_d)
    for tt in range(16):
        S.dma(out_d[tt * 128:(tt + 1) * 128, :], C.x1[:, tt, :])
    S.finalize()
    S.emit()
    S.es.close()
    return nc


def sincos(C, t, n, s_out, c_out, pre_scaled=False, eng="dve"):
    S, A = C.S, C.A
    r, t1 = A.alloc([n], F32)
    t2 = None
    if eng == "pool":
        MAGIC = 12582912.0
        rf, t3 = A.alloc([n], F32)
        if pre_scaled:
            S.ts(rf, t, MAGIC, None, ALU.add, eng=eng)
            src = t
        else:
            S.ts(r, t, 1.0 / TWO_PI, None, ALU.mult, eng=eng)
            S.ts(rf, r, MAGIC, None, ALU.add, eng=eng)
            src = r
        S.ts(rf, rf, -MAGIC, None, ALU.add, eng=eng)
        S.tt(r, src, rf, ALU.subtract, eng=eng)
        A.release(t3)
    else:
        ri, t2 = A.alloc([n], I32)
        if pre_scaled:
            S.copy(ri, t)
            S.copy(r, ri)
            S.tt(r, t, r, ALU.subtract)
        else:
            S.ts(r, t, 1.0 / TWO_PI, None, ALU.mult)
            S.copy(ri, r)
            rf, t3 = A.alloc([n], F32)
            S.copy(rf, ri)
            S.tt(r, r, rf, ALU.subtract)
            A.release(t3)
    S.act(s_out, r, AF.Sin, scale=6.283185)
    h, t4 = A.alloc([n], F32)
    S.act(h, r, AF.Sin, scale=3.1415925)
    S.act(h, h, AF.Square)
    S.act(c_out, h, AF.Identity, bias=C.one_col, scale=-2.0)
    for t_ in (t1, t2, t4):
        if t_ is not None:
            A.release(t_)


def cmul(C, or_, oi_, ar, ai, br, bi, n_shape, eng="dve", neg_i=False):
    S, A = C.S, C.A
    t1, k1 = A.alloc(n_shape, F32)
    t2, k2 = A.alloc(n_shape, F32)
    S.tt(t1, ar, br, ALU.mult, eng=eng)
    S.tt(t2, ai, bi, ALU.mult, eng=eng)
    S.tt(or_, t1, t2, ALU.subtract, eng=eng)
    S.tt(t1, ar, bi, ALU.mult, eng=eng)
    S.tt(t2, ai, br, ALU.mult, eng=eng)
    S.tt(oi_, t1, t2, ALU.add, eng=eng)
    A.release(k1)
    A.release(k2)


def layer_s5(C, x_d):
    S, A, nc, bank, din = C.S, C.A, C.nc, C.bank, C.din
    l = 0
    snap = set(A.live)
    w_in = din("s5_w_in", [1024, 4096])
    w_glu = din("s5_w_glu", [1536, 3072])
    names_L = ["lamr_L", "lami_L", "lstep_L"]
    pL = {}
    tokL, tokR = {}, {}
    for nm in names_L:
        d = din("s5_" + nm, [128, 48])
        pL[nm], tokL[nm] = A.alloc([48], F32)
        S.dma(pL[nm], d)
    for nm in ["B_r_L", "B_i_L", "C_r_L", "C_i_L"]:
        d = din("s5_" + nm, [128, 48, 16])
        pL[nm], tokL[nm] = A.alloc([48, 16], F32)
        S.dma(pL[nm], d)
    pR = {}
    for nm in ["lamr_R", "lami_R", "lstep_R", "B_r_R", "B_i_R"]:
        d = din("s5_" + nm, [128, 12, 64])
        pR[nm], tokR[nm] = A.alloc([12, 64], F32)
        S.dma(pR[nm], d)
    d_d = din("s5_d_col", [128, 12])
    dcol, _ = A.alloc([12], F32)
    S.dma(dcol, d_d)
    cst = {}
    for nm, shp in [("c_maskL", [2]), ("c_maskLn", [2]), ("c_maskR", [2]), ("c_sv", [16]), ("c_jiota", [256])]:
        d = din(nm, [128] + shp)
        cst[nm], _ = A.alloc(shp, F32)
        S.dma(cst[nm], d)
    def setup(P, n, lamr, lami, lstep, eng="dve"):
        delta, k0 = A.alloc([n], F32)
        S.act(delta, lstep, AF.Exp)
        zr, _ = A.alloc([n], F32)
        zi, _ = A.alloc([n], F32)
        S.tt(zr, lamr, delta, ALU.mult, eng=eng)
        S.tt(zi, lami, delta, ALU.mult, eng=eng)
        A.release(k0)
        mag, k1 = A.alloc([n], F32)
        S.act(mag, zr, AF.Exp)
        sn, k2 = A.alloc([n], F32)
        cs, k3 = A.alloc([n], F32)
        sincos(C, zi, n, sn, cs, eng=eng)
        am1, k4 = A.alloc([n], F32)
        ai, k5 = A.alloc([n], F32)
        S.tt(am1, mag, cs, ALU.mult, eng=eng)
        S.ts(am1, am1, -1.0, None, ALU.add, eng=eng)
        S.tt(ai, mag, sn, ALU.mult, eng=eng)
        n2, k6 = A.alloc([n], F32)
        t, k7 = A.alloc([n], F32)
        S.tt(n2, lamr, lamr, ALU.mult, eng=eng)
        S.tt(t, lami, lami, ALU.mult, eng=eng)
        S.tt(n2, n2, t, ALU.add, eng=eng)
        S.recip(n2, n2)
        bf_r, kbr = A.alloc([n], F32)
        bf_i, kbi = A.alloc([n], F32)
        t2, k8 = A.alloc([n], F32)
        S.tt(t, am1, lamr, ALU.mult, eng=eng)
        S.tt(t2, ai, lami, ALU.mult, eng=eng)
        S.tt(t, t, t2, ALU.add, eng=eng)
        S.tt(bf_r, t, n2, ALU.mult, eng=eng)
        S.tt(t, ai, lamr, ALU.mult, eng=eng)
        S.tt(t2, am1, lami, ALU.mult, eng=eng)
        S.tt(t, t, t2, ALU.subtract, eng=eng)
        S.tt(bf_i, t, n2, ALU.mult, eng=eng)
        for k in (k1, k2, k3, k4, k5, k6, k7, k8):
            A.release(k)
        return zr, zi, bf_r, bf_i, kbr, kbi

    zrL, ziL, bfrL, bfiL, kbrL, kbiL = setup(pL, 48, pL["lamr_L"], pL["lami_L"], pL["lstep_L"])
    sv = cst["c_sv"]
    angS, k0 = A.alloc([48, 8], F32)
    magS, k1 = A.alloc([48, 8], F32)
    zi_b = mk(ziL, [(1, 48), (0, 8)])
    zr_b = mk(zrL, [(1, 48), (0, 8)])
    sv_b = mk(sv, [(0, 48), (1, 8)])
    S.tt(angS, zi_b, sv_b, ALU.mult)
    S.tt(magS, zr_b, sv_b, ALU.mult)
    S.act(magS, magS, AF.Exp)
    snS, k2 = A.alloc([48, 8], F32)
    csS, k3 = A.alloc([48, 8], F32)
    fl = lambda a: a.rearrange("p a b -> p (a b)")
    sincos(C, fl(angS), 384, fl(snS), fl(csS))
    powr, _ = A.alloc([48, 8], F32)
    powi, _ = A.alloc([48, 8], F32)
    S.tt(powr, magS, csS, ALU.mult)
    S.tt(powi, magS, snS, ALU.mult)
    for k in (k0, k1, k2, k3):
        A.release(k)
    rho, _ = A.alloc([48], F32)
    S.act(rho, zrL, AF.Exp, scale=8.0)
    phir, _ = A.alloc([48], F32)
    t, k0 = A.alloc([48], F32)
    ti, k1 = A.alloc([48], I32)
    S.ts(t, ziL, 8.0 / TWO_PI, None, ALU.mult)
    S.copy(ti, t)
    S.copy(phir, ti)
    S.tt(phir, t, phir, ALU.subtract)
    A.release(k0)
    A.release(k1)
    BbrL, kb1 = A.alloc([48, 16], F32)
    BbiL, kb2 = A.alloc([48, 16], F32)
    bfr_b = mk(bfrL, [(1, 48), (0, 16)])
    bfi_b = mk(bfiL, [(1, 48), (0, 16)])
    cmul(C, BbrL, BbiL, bfr_b, bfi_b, pL["B_r_L"], pL["B_i_L"], [48, 16])
    BbZ, _ = A.alloc([48, 2, 2, 16], BF16)
    mL = cst["c_maskL"]
    mLn = cst["c_maskLn"]
    for ri, src in ((0, BbrL), (1, BbiL)):
        S.tt(BbZ[:, :, ri, :, :], mk(src, [(16, 48), (0, 2), (1, 16)]), mk(mL, [(0, 48), (1, 2), (0, 16)]), ALU.mult)
    A.release(kb1)
    A.release(kb2)
    for nm in ["B_r_L", "B_i_L", "lamr_L", "lami_L", "lstep_L"]:
        A.release(tokL[nm])
    A.release(kbrL)
    A.release(kbiL)

    fl2 = lambda a: a.rearrange("p a b -> p (a b)")
    zrR, ziR, bfrR, bfiR, kbrR, kbiR = setup(pR, 768, fl2(pR["lamr_R"]), fl2(pR["lami_R"]), fl2(pR["lstep_R"]), eng="pool")
    BbrR, _ = A.alloc([12, 64], F32)
    BbiR, _ = A.alloc([12, 64], F32)
    cmul(C, fl2(BbrR), fl2(BbiR), bfrR, bfiR, fl2(pR["B_r_R"]), fl2(pR["B_i_R"]), [768], eng="pool")
    for nm in tokR:
        A.release(tokR[nm])
    A.release(kbrR)
    A.release(kbiR)
    zrR3 = zrR.rearrange("p (a b) -> p a b", a=12)
    ziR3 = ziR.rearrange("p (a b) -> p a b", a=12)

    hT, k_hT = A.alloc([8, SEQ], BF16)
    ygT, k_yg = A.alloc([12, SEQ], BF16)
    g_ln, k_gln = C.gain_row(C.lng_d, l)
    C.norm_T(x_d, 16, hT, g_ln)
    A.release(k_gln)

    PA = C.WPool(1, 2, 2048)
    S1 = [A.alloc([8, 128], BF16)[0] for _ in range(3)]
    for s1 in S1:
        S.memset(s1, 0.0)
    S2 = [A.alloc([8, 2, 2, 64], BF16)[0] for _ in range(2)]
    S3 = [A.alloc([4, 8, 2, 32], BF16)[0] for _ in range(3)]
    uTs = [A.alloc([SEQ], BF16)[0] for _ in range(2)]
    Fb = [[A.alloc([4, 256], BF16)[0] for _ in range(2)] for _ in range(2)]
    fbuf = [A.alloc([4, 264], F32)[0] for _ in range(2)]
    for fb in fbuf:
        S.memset(fb, 0.0)
    jio = cst["c_jiota"]
    mR = cst["c_maskR"]

    wb_cur = [None]

    TE = "pool"

    def tables(i):
        par = i % 2
        ctr, k1 = A.alloc([4, 8, 16], F32)
        cti, k2 = A.alloc([4, 8, 16], F32)
        q0 = 4 * i
        pr_b = mk(powr[:, q0:q0 + 4, :], [(8, 4), (1, 8), (0, 16)])
        pi_b = mk(powi[:, q0:q0 + 4, :], [(8, 4), (1, 8), (0, 16)])
        cr_b = mk(pL["C_r_L"][:, q0:q0 + 4, :], [(16, 4), (0, 8), (1, 16)])
        ci_b = mk(pL["C_i_L"][:, q0:q0 + 4, :], [(16, 4), (0, 8), (1, 16)])
        cmul(C, ctr, cti, pr_b, pi_b, cr_b, ci_b, [4, 8, 16], eng=TE)
        s3 = S3[i % 3]
        s3v = s3.rearrange("p q s r c -> p (q s) r c")
        for ri, src, msk in ((0, ctr, mL), (1, cti, mLn)):
            S.tt(mk(s3v[:, :, ri, :], [(64, 32), (16, 2), (1, 16)]),
                 mk(src, [(16, 32), (0, 2), (1, 16)]),
                 mk(msk, [(0, 32), (1, 2), (0, 16)]), ALU.mult, eng=TE)
        A.release(k1)
        A.release(k2)
        ang, k1 = A.alloc([8, 64], F32)
        mag, k2 = A.alloc([8, 64], F32)
        svn_b = mk(sv[:, 8:16], [(1, 8), (0, 64)])
        S.tt(ang, mk(ziR3[:, i, :], [(0, 8), (1, 64)]), svn_b, ALU.mult, eng="dve")
        S.tt(mag, mk(zrR3[:, i, :], [(0, 8), (1, 64)]), svn_b, ALU.mult, eng="dve")
        S.act(mag, mag, AF.Exp)
        sn, k3 = A.alloc([8, 64], F32)
        cs, k4 = A.alloc([8, 64], F32)
        sincos(C, fl(ang), 512, fl(sn), fl(cs), eng="dve")
        S.tt(cs, cs, mag, ALU.mult, eng="dve")
        S.tt(sn, sn, mag, ALU.mult, eng="dve")
        btr, k5 = A.alloc([8, 64], F32)
        bti, k6 = A.alloc([8, 64], F32)
        cmul(C, btr, bti, cs, sn, mk(BbrR[:, i, :], [(0, 8), (1, 64)]), mk(BbiR[:, i, :], [(0, 8), (1, 64)]), [8, 64], eng="dve")
        s2 = S2[par]
        for ri, src in ((0, btr), (1, bti)):
            S.tt(s2[:, :, ri, :, :], mk(src, [(64, 8), (0, 2), (1, 64)]), mk(mR, [(0, 8), (1, 2), (0, 64)]), ALU.mult, eng="dve")
        for k in (k1, k2, k3, k4, k5, k6):
            A.release(k)
    Dhi = [A.alloc([128], BF16)[0] for _ in range(3)]
    Dlo = [A.alloc([128], BF16)[0] for _ in range(3)]

    def dtables(i):
        t1, k1 = A.alloc([128], F32)
        t2, k2 = A.alloc([128], F32)
        S.ts(t1, C.identf, dcol[:, i:i + 1], None, ALU.mult, eng=TE)
        S.copy(Dhi[i % 3], t1, eng=TE)
        S.copy(t2, Dhi[i % 3], eng=TE)
        S.tt(t1, t1, t2, ALU.subtract, eng=TE)
        S.copy(Dlo[i % 3], t1, eng=TE)
        A.release(k1)
        A.release(k2)

    def s1gen(i):
        q0 = 4 * i
        s3 = S3[i % 3]
        s1 = S1[i % 3]
        for q in range(4):
            po = C.PS[32 * q:32 * q + 32, 0:256]
            kw = {"tile_position": (0, 96)} if q == 3 else {}
            po3 = po.rearrange("p (t c) -> p t c", t=8)
            S.mm(po3, BbZ[:, q0 + q, 0, :, :].rearrange("p a b -> p (a b)"), s3[:, q, :, 0, :], start=True, stop=False, **kw)
            S.mm(po3, BbZ[:, q0 + q, 1, :, :].rearrange("p a b -> p (a b)"), s3[:, q, :, 1, :], start=False, stop=True, **kw)
            S.copy(s1[32 * q:32 * q + 32, :, 32 * q:32 * q + 32], po3, eng="act")

    def uproj(i):
        par = i % 2
        if i % 2 == 0:
            wb_cur[0] = PA.load(w_in, 8, (i // 2) * 256, 256)
        wb = wb_cur[0]
        lc = (i % 2) * 128
        uT = uTs[par]
        for tb in range(4):
            pb = bank(tb % 2)
            for k in range(8):
                S.mm(pb, wb[:, k, lc:lc + 128], hT[:, k, tb * 512:(tb + 1) * 512], start=(k == 0), stop=(k == 7))
            S.copy(uT.rearrange("p (s j) -> p s j", s=8)[:, :, tb * 64:(tb + 1) * 64],
                   pb.rearrange("p (j s) -> p s j", s=8), eng="act")

    def xprime(i):
        par = i % 2
        uT, s2 = uTs[par], S2[par]
        u3 = uT.rearrange("p (s j) -> p s j", s=8)
        for q in range(4):
            kw = {"tile_position": (96, 0)} if q == 3 else {}
            for ri in range(2):
                po = C.PS[:, 1024 + ri * 1024 + q * 256: 1024 + ri * 1024 + (q + 1) * 256]
                for sp in range(8):
                    S.mm(po, s2[32 * q:32 * q + 32, sp, ri, :, :].rearrange("p a b -> p (a b)"),
                         u3[32 * q:32 * q + 32, sp, :], start=(sp == 0), stop=(sp == 7), **kw)

    rot_sn, _ = A.alloc([4, 256], F32)
    rot_cs, _ = A.alloc([4, 256], F32)

    def rot_tables(i):
        q0 = 4 * i
        r, k1 = A.alloc([4, 256], F32)
        for q in range(4):
            S.ts(r[:, q, :], jio, phir[:, q0 + q:q0 + q + 1], None, ALU.mult)
        sincos(C, fl(r), 1024, fl(rot_sn), fl(rot_cs), pre_scaled=True)
        A.release(k1)

    def chunk_scan(i):
        par = i % 2
        q0 = 4 * i
        fr, fi = fbuf
        Fr, Fi = Fb[par]
        sn, cs = rot_sn, rot_cs
        xr = C.PS[:, 1024:2048].rearrange("p (q j) -> p q j", q=4)
        xi = C.PS[:, 2048:3072].rearrange("p (q j) -> p q j", q=4)
        vr, k4 = A.alloc([4, 256], F32)
        vi, k5 = A.alloc([4, 256], F32)
        t, k6 = A.alloc([4, 256], F32)
        S.tt(vr, xr, cs, ALU.mult)
        S.tt(t, xi, sn, ALU.mult)
        S.tt(vi, xi, cs, ALU.mult)
        S.tt(vr, vr, t, ALU.add)
        S.tt(t, xr, sn, ALU.mult)
        S.tt(vi, vi, t, ALU.subtract)
        for q in range(4):
            rb = mk(rho[:, q0 + q:q0 + q + 1], [(0, 256)])
            S.scan(fr[:, q, 1:257], vr[:, q, :], rb, 0.0, ALU.add, ALU.mult)
            S.scan(fi[:, q, 1:257], vi[:, q, :], rb, 0.0, ALU.add, ALU.mult)
        frh = fr[:, :, 0:256]
        fih = fi[:, :, 0:256]
        S.tt(vr, frh, cs, ALU.mult)
        S.tt(t, fih, sn, ALU.mult)
        S.tt(Fr, vr, t, ALU.subtract)
        S.tt(vi, frh, sn, ALU.mult)
        S.tt(t, fih, cs, ALU.mult)
        S.tt(Fi, vi, t, ALU.add)
        for k in (k4, k5, k6):
            A.release(k)

    def youts(i):
        par = i % 2
        uT, s1, s3 = uTs[par], S1[i % 3], S3[i % 3]
        Fr, Fi = Fb[par]
        u3 = uT.rearrange("p (s j) -> p s j", s=8)
        yg3 = ygT[:, i, :].rearrange("p (j s) -> p s j", s=8)
        for sp in range(4):
            b = 6 + sp % 2
            for s in (2 * sp, 2 * sp + 1):
                po = bank(b, 256, (s % 2) * 256)
                for s_ in range(s + 1):
                    S.mm(po, s1[:, s - s_, :], u3[:, s_, :], start=(s_ == 0), stop=False)
                S.mm(po, Dhi[i % 3], u3[:, s, :], start=False, stop=False)
                S.mm(po, Dlo[i % 3], u3[:, s, :], start=False, stop=False)
                for q in range(4):
                    kw = {"tile_position": (0, 96)} if q == 3 else {}
                    pq = C.PS[32 * q:32 * q + 32, 512 * b + (s % 2) * 256: 512 * b + (s % 2) * 256 + 256]
                    S.mm(pq, s3[:, q, s, 0, :], Fr[:, q, :], start=False, stop=False, **kw)
                    S.mm(pq, s3[:, q, s, 1, :], Fi[:, q, :], start=False, stop=(q == 3), **kw)
            S.act(yg3[:, 2 * sp:2 * sp + 2, :], bank(b).rearrange("p (s j) -> p s j", s=2), AF.Gelu)

    def dump(name, ap, shape_free, src_dt=F32):
        n = 1
        for d_ in shape_free:
            n *= d_
        tmp, kk = A.alloc([n], F32)
        S.copy(tmp, ap if len(shape_free) == 1 else ap)
        C.dbg_out(name, tmp, [128, n])
        A.release(kk)

    tables(0)
    dtables(0)
    s1gen(0)
    tables(1)
    dtables(1)
    uproj(0)
    xprime(0)
    rot_tables(0)
    for i in range(12):
        if i == 0 and "t0" in C.dbg:
            dump("hT0", hT[:, 0, :], [SEQ])
            dump("u0", uTs[0], [SEQ])
            dump("S1", S1[0].rearrange("p a b -> p (a b)"), [1024])
            dump("S2", S2[0].rearrange("p a b c d -> p (a b c d)"), [2048])
            dump("S3", S3[0].rearrange("p a b c d -> p (a b c d)"), [2048])
            pass
            dump("powr", powr.rearrange("p a b -> p (a b)"), [384])
            dump("powi", powi.rearrange("p a b -> p (a b)"), [384])
            dump("rho", rho, [48])
            dump("phir", phir, [48])
            dump("bbz", BbZ.rearrange("p a b c d -> p (a b c d)"), [48 * 64])
        chunk_scan(i)
        if i + 1 < 12:
            uproj(i + 1)
            xprime(i + 1)
        if i + 2 < 12:
            tables(i + 2)
            dtables(i + 2)
        if i + 1 < 12:
            rot_tables(i + 1)
        youts(i)
        if i + 1 < 12:
            s1gen(i + 1)
    if "yg" in C.dbg:
        for i in range(12):
            tmp, kk = A.alloc([SEQ], F32)
            S.copy(tmp, ygT[:, i, :])
            C.dbg_out("yg%d" % i, tmp, [128, SEQ])
            A.release(kk)

    for k in list(A.live):
        if k not in snap and k not in (k_hT, k_yg):
            A.release(k)
    oT, k_oT = A.alloc([16, SEQ], BF16)
    gate_c0 = 2048

    def gate_mul(i, tb, val_f32, wg, lc):
        pg = bank(4 + (C.gcnt[0] % 2))
        C.gcnt[0] += 1
        for k in range(8):
            S.mm(pg, wg[:, k, lc:lc + 128], hT[:, k, tb * 512:(tb + 1) * 512], start=(k == 0), stop=(k == 7))
        C.silu_gate(oT[:, i, tb * 512:(tb + 1) * 512], val_f32, pg)
    C.gcnt = [0]

    PB = C.WPool(2, 6, 1536)
    for i in range(12):
        wa = PB.load(w_glu, 12, i * 128, 128)
        wb_ = PB.load(w_glu, 12, 1536 + i * 128, 128)
        wg = PB.load(w_in, 8, gate_c0 + i * 128, 128)
        if True:
            lc = 0
            for tb in range(4):
                pa = bank(0 + tb % 2)
                pb_ = bank(2 + tb % 2)
                for k in range(12):
                    S.mm(pa, wa[:, k, lc:lc + 128], ygT[:, k, tb * 512:(tb + 1) * 512], start=(k == 0), stop=(k == 11))
                for k in range(12):
                    S.mm(pb_, wb_[:, k, lc:lc + 128], ygT[:, k, tb * 512:(tb + 1) * 512], start=(k == 0), stop=(k == 11))
                sig, k1 = A.alloc([512], F32)
                C.sigmoid(sig, pb_)
                mix, k2 = A.alloc([512], F32)
                S.tt(mix, pa, sig, ALU.mult)
                gate_mul(i, tb, mix, wg, lc)
                A.release(k1)
                A.release(k2)
    A.release(k_yg)
    PB.free()

    wo_pre = A.alloc([16, 1024], BF16)
    P0 = C.WPool(2, 0, 64)
    for cb in range(8):
        P0.load(C.wout_d[l], 16, cb * 128, 128, dst=wo_pre[0][:, :, cb * 128:(cb + 1) * 128])
    mem_attention(C, l, w_in, 1536, gate_c0 + 1536, None, hT, oT, gate_mul)
    A.release(k_hT)
    C.x1, _ = A.alloc([16, 1024], F32)

    out_proj(C, l, oT, x_src_d=x_d, wo_pre=wo_pre)
    A.release(k_oT)


def mem_item(C, l, h, tb, wq, hT, KmT, Vm, wg, wg_c0, dst, bset):
    S, A, bank = C.S, C.A, C.bank
    b0, b1, b2, b3 = [bank(b) for b in bset]
    scale = 128.0 ** -0.5
    st = {}

    def s1():
        for k in range(8):
            S.mm(b0, wq[:, k, h * 128:(h + 1) * 128], hT[:, k, tb * 512:(tb + 1) * 512], start=(k == 0), stop=(k == 7))
        st["sq"], st["k_sq"] = A.alloc([512], BF16)
        S.act(st["sq"], b0, AF.Square)

    def s2():
        S.mm(b1, C.ones, st["sq"])
        A.release(st["k_sq"])
        rs, k2 = A.alloc([512], F32)
        C.rstd(rs, b1, 1.0 / 128)
        st["xqn"], st["k_xqn"] = A.alloc([512], BF16)
        S.stt(st["xqn"], b0, C.xqn[:, l:l + 1], rs, ALU.mult, ALU.mult)
        A.release(k2)

    def s3():
        for kt, bb in ((0, b0), (1, b1)):
            S.mm(bb, KmT[:, h, kt * 128:(kt + 1) * 128], st["xqn"])
            st["pT", kt], st["k_pT", kt] = A.alloc([512], BF16)
            S.act(st["pT", kt], bb, AF.Exp, scale=scale)
        A.release(st["k_xqn"])

    def s4():
        for kt in range(2):
            S.mm(b2, Vm[:, kt, h * 128:(h + 1) * 128], st["pT", kt], start=(kt == 0), stop=(kt == 1))
            S.mm(b3, C.ones, st["pT", kt], start=(kt == 0), stop=(kt == 1))
            A.release(st["k_pT", kt])
        st["rinv"], st["k_rinv"] = A.alloc([512], F32)
        st["mo"], st["k_mo"] = A.alloc([512], F32)
        S.act(st["rinv"], b3, AF.Ln)
        S.copy(st["mo"], b2, eng="act")
        S.act(st["rinv"], st["rinv"], AF.Exp, scale=-1.0)

    def s5():
        S.tt(st["mo"], st["mo"], st["rinv"], ALU.mult)
        A.release(st["k_rinv"])
        for k in range(8):
            S.mm(b0, wg[:, k, wg_c0:wg_c0 + 128], hT[:, k, tb * 512:(tb + 1) * 512], start=(k == 0), stop=(k == 7))
        C.silu_gate(dst, st["mo"], b0)
        A.release(st["k_mo"])
    return [s1, s2, s3, s4, s5]


def run_interleaved(items, k=2):
    items = list(items)
    live = []
    nxt = 0
    while live or nxt < len(items):
        while len(live) < k and nxt < len(items):
            live.append(list(items[nxt]))
            nxt += 1
        for it in list(live):
            it.pop(0)()
            if not it:
                live.remove(it)


def mem_attention(C, l, w_in, xq_c0, gate_c0, gain, hT, oT, gate_mul):
    S, A, bank = C.S, C.A, C.bank
    P = C.WPool(1, 4, 4096)
    KmT, Vm, tk, tv = C.mem_kv(l, P)
    wq = P.load(w_in, 8, xq_c0, 512)
    wg = P.load(w_in, 8, gate_c0, 512)
    items = []
    n = 0
    for h in range(4):
        for tb in range(4):
            bset = (0, 1, 2, 3) if n % 2 == 0 else (4, 5, 6, 7)
            items.append(mem_item(C, l, h, tb, wq, hT, KmT, Vm, wg, h * 128, oT[:, 12 + h, tb * 512:(tb + 1) * 512], bset))
            n += 1
    run_interleaved(items, 2)
    A.release(tk)
    A.release(tv)
    P.free()


def out_proj(C, l, oT, x_src_d=None, wo_pre=None):
    S, A, bank = C.S, C.A, C.bank
    P = C.WPool(2, 0, 64)
    if wo_pre is not None:
        wo, kwo = wo_pre
    else:
        wo, kwo = A.alloc([16, 1024], BF16)
        for cb in range(8):
            P.load(C.wout_d[l], 16, cb * 128, 128, dst=wo[:, :, cb * 128:(cb + 1) * 128])
    for tt in range(16):
        if x_src_d is not None:
            xt, kx = A.alloc([1024], F32)
            S.dma(xt, x_src_d[tt * 128:(tt + 1) * 128, :])
        else:
            xt = C.x1[:, tt, :]
        for dh in range(2):
            po = bank(dh + 2 * (tt % 2))
            for k in range(16):
                S.mm(po, oT[:, k, tt * 128:(tt + 1) * 128], wo[:, k, dh * 512:(dh + 1) * 512], start=(k == 0), stop=(k == 15))
            S.tt(C.x1[:, tt, dh * 512:(dh + 1) * 512], po, xt[:, dh * 512:(dh + 1) * 512], ALU.add)
        if x_src_d is not None:
            A.release(kx)
    A.release(kwo)
    P.free()


def out_proj_group(C, l, oG, kt0, nk, P):
    S, A, bank = C.S, C.A, C.bank
    wo = P.load(C.wout_d[l][kt0 * 128:(kt0 + nk) * 128, :], nk, 0, 1024)
    for tt in range(16):
        for dh in range(2):
            po = bank(dh)
            for k in range(nk):
                S.mm(po, oG[:, k, tt * 128:(tt + 1) * 128], wo[:, k, dh * 512:(dh + 1) * 512], start=(k == 0), stop=(k == nk - 1))
            xs = C.x1[:, tt, dh * 512:(dh + 1) * 512]
            S.tt(xs, po, xs, ALU.add)


def layer_mla(C, pos_d):
    S, A, nc, bank, din = C.S, C.A, C.nc, C.bank, C.din
    l = 1
    snap0 = set(A.live)
    w_in = din("mla_w_in_ext", [1024, 3584])
    w_uq = din("mla_w_uq_ext", [512, 3072])
    w_uk = din("mla_w_ukv_k", [256, 1536])
    w_uv = din("mla_w_ukv_v", [256, 1536])
    CQ0, CKV0, XQ0, G0, KR0, KRS0 = 0, 512, 768, 1280, 3328, 3456
    prm, prm_tok = {}, {}
    for nm, shp in [("qlg", [4]), ("kvlg", [2]), ("qng", [1]), ("kng", [1]), ("qrg", [2]), ("krg", [2]),
                    ("invf", [1]), ("sgn", [1]), ("tri", [128]), ("bones", [128])]:
        d = din("mla_" + nm, [128] + shp)
        prm[nm], prm_tok[nm] = A.alloc(shp, F32)
        S.dma(prm[nm], d)
    tri, _ = A.alloc([128], BF16)
    S.copy(tri, prm["tri"])
    bones, _ = A.alloc([128], BF16)
    S.copy(bones, prm["bones"])
    A.release(prm_tok["tri"])
    A.release(prm_tok["bones"])
    scale = 192.0 ** -0.5

    hT, k_hT = A.alloc([8, SEQ], BF16)
    P = C.WPool(1, 2, 4096)
    wk_pre = P.load(C.wmem_d[l], 8, 0, 512)
    wv_pre = P.load(C.wmem_d[l], 8, 512, 512)
    wq_buf, k_wq = A.alloc([8, 512], BF16)
    wq = P.load(w_in, 8, XQ0, 512, dst=wq_buf)
    g_ln, k_gln = C.gain_row(C.lng_d, l)
    cosT, _ = A.alloc([SEQ], F32)
    sinT, _ = A.alloc([SEQ], F32)
    posi, k_posi = A.alloc([SEQ], I32)
    S.dma(posi, pos_d.partition_broadcast(128))
    ang, k_ang = A.alloc([SEQ], F32)
    r_, k_r = A.alloc([SEQ], F32)
    ri_, k_ri = A.alloc([SEQ], I32)
    h_, k_h = A.alloc([SEQ], F32)
    rope_ops = [
        lambda: S.copy(ang, posi),
        lambda: S.ts(ang, ang, prm["invf"][:, 0:1], None, ALU.mult),
        lambda: S.ts(r_, ang, 1.0 / TWO_PI, None, ALU.mult),
        lambda: S.copy(ri_, r_),
        lambda: S.copy(ang, ri_),
        lambda: S.tt(r_, r_, ang, ALU.subtract),
        lambda: S.act(sinT, r_, AF.Sin, scale=6.283185),
        lambda: S.act(h_, r_, AF.Sin, scale=3.1415925),
        lambda: S.act(h_, h_, AF.Square),
        lambda: S.act(cosT, h_, AF.Identity, bias=C.one_col, scale=-2.0),
        lambda: S.ts(sinT, sinT, prm["sgn"][:, 0:1], None, ALU.mult),
    ]
    for tt in range(16):
        C.norm_T_tile(C.x1[:, tt, :], tt, hT, g_ln, last=(tt == 15))
        if rope_ops:
            rope_ops.pop(0)()
    while rope_ops:
        rope_ops.pop(0)()
    for k_ in (k_posi, k_ang, k_r, k_ri, k_h):
        A.release(k_)
    A.release(k_gln)

    def proj_ps(pb, w, nk, c0, actT, tb):
        for k in range(nk):
            S.mm(pb, w[:, k, c0:c0 + 128], actT[:, k, tb * 512:(tb + 1) * 512], start=(k == 0), stop=(k == nk - 1))

    gcnt = [0]

    def gate_mul(dst, tb, val_f32, wg, lc):
        pg = bank(gcnt[0] % 2)
        gcnt[0] += 1
        proj_ps(pg, wg, 8, lc, hT, tb)
        C.silu_gate(dst, val_f32, pg)

    KmT, Vm, tk, tv = C.mem_kv(l, P, wk_pre, wv_pre)
    mscale = 128.0 ** -0.5
    for hp in range(2):
        oG, k_oG = A.alloc([2, SEQ], BF16)
        wg = P.load(w_in, 8, G0 + (12 + 2 * hp) * 128, 256)
        items = []
        n = 0
        for hh in range(2):
            h = 2 * hp + hh
            for tb in range(4):
                bset = (0, 1, 2, 3) if n % 2 == 0 else (4, 5, 6, 7)
                items.append(mem_item(C, l, h, tb, wq, hT, KmT, Vm, wg, hh * 128, oG[:, hh, tb * 512:(tb + 1) * 512], bset))
                n += 1
        run_interleaved(items, 2)
        if "l1" in C.dbg and hp == 0:
            for hh in range(2):
                t_, k_ = A.alloc([SEQ], F32)
                S.copy(t_, oG[:, hh, :])
                C.dbg_out("omem%d" % hh, t_, [128, SEQ])
                A.release(k_)
        out_proj_group(C, l, oG, 12 + 2 * hp, 2, P)
        A.release(k_oG)
    A.release(tk)
    A.release(tv)
    A.release(k_wq)


    cqn, _ = A.alloc([4, SEQ], BF16)
    ckvn, _ = A.alloc([2, SEQ], BF16)

    def lora_items(c0, nt, dst, gcols):
        w = P.load(w_in, 8, c0, nt * 128)
        items = []
        for tb in range(4):
            def mk_item(tb=tb):
                st = {}
                bset = (0, 1, 2) if tb % 2 == 0 else (4, 5, 6)

                def sa(m):
                    if m == 0:
                        st["cf"], st["k_cf"] = A.alloc([nt, 512], F32)
                    pb = bank(bset[m % 2])
                    proj_ps(pb, w, 8, m * 128, hT, tb)
                    S.copy(st["cf"][:, m, :], pb, eng="act")
                    sq, k2 = A.alloc([512], BF16)
                    S.act(sq, pb, AF.Square)
                    S.mm(bank(bset[2]), C.ones, sq, start=(m == 0), stop=(m == nt - 1))
                    A.release(k2)

                def sb():
                    rs, k3 = A.alloc([512], F32)
                    C.rstd(rs, bank(bset[2]), 1.0 / (nt * 128))
                    for m in range(nt):
                        S.stt(dst[:, m, tb * 512:(tb + 1) * 512], st["cf"][:, m, :], gcols[:, m:m + 1], rs, ALU.mult, ALU.mult)
                    A.release(st["k_cf"])
                    A.release(k3)
                return [(lambda m=m: sa(m)) for m in range(nt)] + [sb]
            items.append(mk_item())
        return items
    run_interleaved(lora_items(CQ0, 4, cqn, prm["qlg"]), 2)
    run_interleaved(lora_items(CKV0, 2, ckvn, prm["kvlg"]), 2)

    def rope_tile(pp, ps_, gcol, gscol, dst, tb, psb):
        sq, k1 = A.alloc([512], BF16)
        S.act(sq, pp, AF.Square)
        S.mm(psb, bones, sq)
        A.release(k1)
        rs, k2 = A.alloc([512], F32)
        C.rstd(rs, psb, 1.0 / 64)
        b, k4 = A.alloc([512], F32)
        S.stt(b, ps_, gscol, rs, ALU.mult, ALU.mult)
        S.stt(rs, pp, gcol, rs, ALU.mult, ALU.mult)
        S.tt(rs, rs, cosT[:, tb * 512:(tb + 1) * 512], ALU.mult)
        S.tt(b, b, sinT[:, tb * 512:(tb + 1) * 512], ALU.mult)
        S.tt(dst, rs, b, ALU.add)
        for k in (k2, k4):
            A.release(k)

    krT, _ = A.alloc([SEQ], BF16)
    wkr = P.load(w_in, 8, KR0, 256)
    for tb in range(4):
        pp, ps_ = bank(0), bank(1)
        proj_ps(pp, wkr, 8, 0, hT, tb)
        proj_ps(ps_, wkr, 8, 128, hT, tb)
        rope_tile(pp, ps_, prm["krg"][:, 0:1], prm["krg"][:, 1:2], krT[:, tb * 512:(tb + 1) * 512], tb, bank(2))

    P.free()
    P = C.WPool(1, 0, 64)
    wg_ring = [A.alloc([8, 128], BF16)[0] for _ in range(2)]
    wbuf = [{"wr": A.alloc([4, 128], BF16)[0], "wrs": A.alloc([4, 128], BF16)[0], "wqn": A.alloc([4, 128], BF16)[0],
             "wkn": A.alloc([2, 128], BF16)[0], "wv": A.alloc([2, 128], BF16)[0]} for _ in range(2)]
    wo_buf, _ = A.alloc([2, 1024], BF16)
    pairs, heads = {}, {}
    tcnt = [0]

    def tbank():
        tcnt[0] += 1
        return bank(3 + tcnt[0] % 2)
    pending = []

    def gate_mul4(dst, tb, val_f32, wg):
        pg = tbank()
        proj_ps(pg, wg, 8, 0, hT, tb)
        C.silu_gate(dst, val_f32, pg)

    def prep_units(h):
        hp, hh = divmod(h, 2)
        W = wbuf[h % 2]
        st = {}
        units = []

        def u_alloc():
            if hh == 0:
                qr, k_qr = A.alloc([SEQ], BF16)
                pairs[hp] = {"qr": qr, "k_qr": k_qr}
                P.load(w_uq, 4, 1536 + hp * 128, 128, dst=W["wr"])
                P.load(w_uq, 4, 2304 + hp * 128, 128, dst=W["wrs"])
            qn, k_qn = A.alloc([SEQ], BF16)
            kn, k_kn = A.alloc([SEQ], BF16)
            Vh, k_V = A.alloc([16, 128], BF16)
            P.load(w_uq, 4, h * 128, 128, dst=W["wqn"])
            P.load(w_uk, 2, h * 128, 128, dst=W["wkn"])
            P.load(w_uv, 2, h * 128, 128, dst=W["wv"])
            wg = P.load(w_in, 8, G0 + h * 128, 128, dst=wg_ring[h % 2])
            heads[h] = {"qn": qn, "kn": kn, "Vh": Vh, "wg": wg, "k_qn": k_qn, "k_kn": k_kn, "k_V": k_V}
        units.append(u_alloc)

        def r1(tb):
            pp, ps_ = bank(0), bank(1)
            proj_ps(pp, W["wr"], 4, 0, cqn, tb)
            proj_ps(ps_, W["wrs"], 4, 0, cqn, tb)
            sq, k1 = A.alloc([512], BF16)
            S.act(sq, pp, AF.Square)
            st["r", tb] = (sq, k1)

        def r2(tb):
            pp, ps_ = bank(0), bank(1)
            sq, k1 = st.pop(("r", tb))
            psb = tbank()
            S.mm(psb, bones, sq)
            A.release(k1)
            rs, k2 = A.alloc([512], F32)
            C.rstd(rs, psb, 1.0 / 64)
            b, k4 = A.alloc([512], F32)
            S.stt(b, ps_, prm["qrg"][:, 1:2], rs, ALU.mult, ALU.mult)
            S.stt(rs, pp, prm["qrg"][:, 0:1], rs, ALU.mult, ALU.mult)
            S.tt(rs, rs, cosT[:, tb * 512:(tb + 1) * 512], ALU.mult)
            S.tt(b, b, sinT[:, tb * 512:(tb + 1) * 512], ALU.mult)
            S.tt(pairs[hp]["qr"][:, tb * 512:(tb + 1) * 512], rs, b, ALU.add)
            A.release(k2)
            A.release(k4)

        def n1(kind, tb):
            pb = bank(tb % 2)
            if kind == "q":
                proj_ps(pb, W["wqn"], 4, 0, cqn, tb)
            else:
                proj_ps(pb, W["wkn"], 2, 0, ckvn, tb)
            sq, k1 = A.alloc([512], BF16)
            S.act(sq, pb, AF.Square)
            st[kind, tb] = (sq, k1)

        def n2(kind, tb):
            pb = bank(tb % 2)
            sq, k1 = st.pop((kind, tb))
            psb = tbank()
            S.mm(psb, C.ones, sq)
            A.release(k1)
            rs, k2 = A.alloc([512], F32)
            C.rstd(rs, psb, 1.0 / 128)
            dst = heads[h]["qn" if kind == "q" else "kn"][:, tb * 512:(tb + 1) * 512]
            S.stt(dst, pb, prm["qng" if kind == "q" else "kng"][:, 0:1], rs, ALU.mult, ALU.mult)
            A.release(k2)

        def vv(t4):
            pb = bank(t4 % 2)
            Vh = heads[h]["Vh"]
            for j in range(4):
                tt = 4 * t4 + j
                for k in range(2):
                    S.mm(pb[:, j * 128:(j + 1) * 128], ckvn[:, k, tt * 128:(tt + 1) * 128], W["wv"][:, k, :], start=(k == 0), stop=(k == 1))
            S.copy(Vh[:, 4 * t4:4 * t4 + 4, :], pb.rearrange("p (j d) -> p j d", j=4), eng="act")

        if hh == 0:
            for tb in range(4):
                units.append(lambda tb=tb: r1(tb))
                units.append(lambda tb=tb: r2(tb))
        for kind in ("q", "k"):
            for tb in range(4):
                units.append(lambda kind=kind, tb=tb: n1(kind, tb))
                units.append(lambda kind=kind, tb=tb: n2(kind, tb))
        for t4 in range(4):
            units.append(lambda t4=t4: vv(t4))
        return units

    def outproj_units(hp, oe, oo, toks):
        units = []

        def u_load():
            P.load(C.wout_d[l][2 * hp * 128:(2 * hp + 2) * 128, :], 2, 0, 1024, dst=wo_buf)
        units.append(u_load)

        def u(tt, dh):
            po = tbank()
            S.mm(po, oe[:, tt * 128:(tt + 1) * 128], wo_buf[:, 0, dh * 512:(dh + 1) * 512], start=True, stop=False)
            S.mm(po, oo[:, tt * 128:(tt + 1) * 128], wo_buf[:, 1, dh * 512:(dh + 1) * 512], start=False, stop=True)
            xs = C.x1[:, tt, dh * 512:(dh + 1) * 512]
            S.tt(xs, po, xs, ALU.add)
        for tt in range(16):
            for dh in range(2):
                units.append(lambda tt=tt, dh=dh: u(tt, dh))

        def u_rel():
            for k in toks:
                A.release(k)
        units.append(u_rel)
        return units

    def attn(h):
        hp, hh = divmod(h, 2)
        hb = 64 * hh
        H = heads[h]
        qn, kn, Vh, wg = H["qn"], H["kn"], H["Vh"], H["wg"]
        qr = pairs[hp]["qr"]
        items = []
        for qb in range(4):
            nkt = 4 * qb + 4
            for kt in range(nkt):
                items.append((qb, kt, nkt))
        pst_of = {}

        def stage_qk(idx):
            qb, kt, nkt = items[idx]
            j = kt - 4 * qb
            c0 = 128 * j if j > 0 else 0
            n = 512 - c0
            pst = bank(6 + idx % 2)
            q0 = qb * 512 + c0
            S.mm(pst[:, 0:n], kn[:, kt * 128:(kt + 1) * 128], qn[:, q0:q0 + n], start=True, stop=False)
            S.mm(pst[:, 0:n], krT[hb:hb + 64, kt * 128:(kt + 1) * 128], qr[hb:hb + 64, q0:q0 + n], start=False, stop=True)
            pT, k2 = A.alloc([512], BF16)
            S.act(pT[:, 0:n], pst[:, 0:n], AF.Exp, scale=scale)
            if j >= 0:
                S.tt(pT[:, 0:128], pT[:, 0:128], tri, ALU.mult)
            pst_of[idx] = (pT, k2, c0, n)

        def stage_pv(idx):
            qb, kt, nkt = items[idx]
            pT, k2, c0, n = pst_of.pop(idx)
            pacc, psum_ = bank(2), bank(5)
            S.mm(pacc[:, c0:512], Vh[:, kt, :], pT[:, 0:n], start=(kt == 0), stop=(kt == nkt - 1))
            S.mm(psum_[:, c0:512], C.ones, pT[:, 0:n], start=(kt == 0), stop=(kt == nkt - 1))
            A.release(k2)
            if kt == nkt - 1:
                rinv, k3 = A.alloc([512], F32)
                mo, k4 = A.alloc([512], F32)
                S.act(rinv, psum_, AF.Ln)
                S.copy(mo, pacc, eng="act")
                S.act(rinv, rinv, AF.Exp, scale=-1.0)
                S.tt(mo, mo, rinv, ALU.mult)
                gate_mul4(qn[:, qb * 512:(qb + 1) * 512], qb, mo, wg)
                A.release(k3)
                A.release(k4)

        npop = -(-len(pending) // (len(items) - 2))
        stage_qk(0)
        for idx in range(len(items)):
            if idx + 1 < len(items):
                stage_qk(idx + 1)
            stage_pv(idx)
            for _ in range(npop):
                if pending:
                    pending.pop(0)()
        while pending:
            pending.pop(0)()
        A.release(H["k_kn"])
        A.release(H["k_V"])
        if hh == 1:
            He = heads[h - 1]
            A.release(pairs[hp]["k_qr"])
            pending.extend(outproj_units(hp, He["qn"], qn, (He["k_qn"], H["k_qn"])))

    for u_ in prep_units(0):
        u_()
    for h in range(12):
        if h + 1 < 12:
            pending.extend(prep_units(h + 1))
        attn(h)
    while pending:
        pending.pop(0)()
    P.free()
    for k in list(A.live):
        if k not in snap0:
            A.release(k)


def pack_shared(inp):
    f = np.float32
    sh = {}
    sh["c_ident"] = np.eye(128, dtype=f)
    lane = np.arange(128)
    sh["c_maskL"] = np.stack([(lane // 64 == 0), (lane // 64 == 1)], 1).astype(f)
    sh["c_maskLn"] = -sh["c_maskL"]
    g2r = (lane // 16) % 2
    sh["c_maskR"] = np.stack([(g2r == 0), (g2r == 1)], 1).astype(f)
    sv = np.concatenate([np.arange(8), -np.arange(8)]).astype(f)
    sh["c_sv"] = np.broadcast_to(sv, (128, 16)).copy()
    sh["c_jiota"] = np.broadcast_to(np.arange(256, dtype=f), (128, 256)).copy()
    sh["ln_gain"] = inp["ln_gain"]
    sh["mem_norm"] = inp["mem_norm"]
    sh["p_xq_norm"] = np.ascontiguousarray(inp["xq_norm"].T)
    sh["p_xk_norm"] = np.ascontiguousarray(inp["xk_norm"].T)
    sh["w_out"] = inp["w_out"]
    sh["w_mem_kv"] = inp["w_mem_kv"]
    sh["s5_w_in"] = inp["s5_w_in"][0]
    sh["s5_w_glu"] = inp["s5_w_glu"][0]
    lamr, lami, lst = inp["s5_lambda_re"][0], inp["s5_lambda_im"][0], inp["s5_log_step"][0]
    toL = lambda a: np.ascontiguousarray(a.reshape(48, 2, 64).transpose(1, 2, 0).reshape(128, 48))
    sh["s5_lamr_L"] = toL(lamr)
    sh["s5_lami_L"] = toL(lami)
    sh["s5_lstep_L"] = toL(np.broadcast_to(lst[:, None], (96, 64)))
    toL3 = lambda a: np.ascontiguousarray(a.reshape(48, 2, 64, 16).transpose(1, 2, 0, 3).reshape(128, 48, 16))
    sh["s5_B_r_L"] = toL3(inp["s5_b_re"][0])
    sh["s5_B_i_L"] = toL3(inp["s5_b_im"][0])
    sh["s5_C_r_L"] = toL3(inp["s5_c_re"][0].transpose(0, 2, 1))
    sh["s5_C_i_L"] = toL3(inp["s5_c_im"][0].transpose(0, 2, 1))
    def toR(a):
        return np.ascontiguousarray(a.reshape(12, 4, 2, 64, 16).transpose(1, 2, 4, 0, 3).reshape(128, 12, 64))
    rep = lambda a: np.broadcast_to(a[:, :, None], (96, 64, 16))
    sh["s5_lamr_R"] = toR(rep(lamr))
    sh["s5_lami_R"] = toR(rep(lami))
    sh["s5_lstep_R"] = toR(rep(np.broadcast_to(lst[:, None], (96, 64))))
    sh["s5_B_r_R"] = toR(inp["s5_b_re"][0])
    sh["s5_B_i_R"] = toR(inp["s5_b_im"][0])
    sh["s5_d_col"] = np.ascontiguousarray(inp["s5_d"][0].reshape(12, 128).T)
    wi = inp["mla_w_in"][0]
    kr = wi[:, 768:832]
    krs = np.concatenate([kr[:, 32:], kr[:, :32]], 1)
    sh["mla_w_in_ext"] = np.concatenate([wi[:, 0:768], wi[:, 832:3392], kr, kr, krs, krs], 1)
    wq = inp["mla_w_uq"][0].reshape(512, 12, 192)
    nope = wq[:, :, :128].reshape(512, 1536)
    ropep = wq[:, :, 128:]
    ropes = np.concatenate([ropep[:, :, 32:], ropep[:, :, :32]], 2)
    sh["mla_w_uq_ext"] = np.concatenate([nope, ropep.reshape(512, 768), ropes.reshape(512, 768)], 1)
    wkv = inp["mla_w_ukv"][0].reshape(256, 12, 256)
    sh["mla_w_ukv_k"] = wkv[:, :, :128].reshape(256, 1536)
    sh["mla_w_ukv_v"] = wkv[:, :, 128:].reshape(256, 1536)
    sh["mla_qlg"] = inp["mla_q_lora_norm"][0].reshape(4, 128).T
    sh["mla_kvlg"] = inp["mla_kv_lora_norm"][0].reshape(2, 128).T
    sh["mla_qng"] = inp["mla_q_nope_norm"][0].reshape(128, 1)
    sh["mla_kng"] = inp["mla_k_nope_norm"][0].reshape(128, 1)
    def ropeg(gv):
        gs = np.concatenate([gv[32:], gv[:32]])
        return np.stack([np.concatenate([gv, gv]), np.concatenate([gs, gs])], 1)
    sh["mla_qrg"] = ropeg(inp["mla_q_rope_norm"][0])
    sh["mla_krg"] = ropeg(inp["mla_k_rope_norm"][0])
    half = 32
    invf = (10000.0 ** (-np.arange(half, dtype=np.float32) / half)).astype(f)
    p64 = lane % 64
    sh["mla_invf"] = invf[p64 % 32].reshape(128, 1)
    sh["mla_sgn"] = np.where(p64 < 32, -1.0, 1.0).astype(f).reshape(128, 1)
    kk = np.arange(128)
    sh["mla_tri"] = (kk[:, None] <= kk[None, :]).astype(f)
    sh["mla_bones"] = (kk[:, None] // 64 == kk[None, :] // 64).astype(f)
    return {k: np.ascontiguousarray(v, dtype=v.dtype) for k, v in sh.items()}


_NC_CACHE = {}


def kernel(**inputs):
    inp = {k: np.asarray(v) for k, v in inputs.items()}
    sh = pack_shared(inp)
    if "nc" not in _NC_CACHE:
        _NC_CACHE["nc"] = build()
    nc = _NC_CACHE["nc"]
    in_maps = []
    for b in range(8):
        m = dict(sh)
        m["x"] = np.ascontiguousarray(inp["x"][b], dtype=np.float32)
        m["mem"] = np.ascontiguousarray(inp["mem"][b], dtype=np.float32)
        m["pos"] = np.ascontiguousarray(inp["positions"][b:b + 1], dtype=np.int32)
        in_maps.append(m)
    res = run_bass_kernel_spmd(nc, in_maps, core_ids=list(range(8)))
    return np.stack([np.asarray(r["out"], dtype=np.float32) for r in res.results], 0)
```

```python
from concourse.bass_utils import run_bass_kernel_spmd
import numpy as np
from contextlib import ExitStack
import concourse.bass as bass
import concourse.mybir as mybir

F32 = mybir.dt.float32
BF16 = mybir.dt.bfloat16
I32 = mybir.dt.int32
ALU = mybir.AluOpType
AF = mybir.ActivationFunctionType
AX = mybir.AxisListType

ENGS = ("pe", "act", "dve", "pool", "sp")


_DTS = {}


def _dsize(dt):
    k = str(dt)
    v = _DTS.get(k)
    if v is None:
        v = 2 if ("16" in k) else (1 if "8" in k else (8 if "64" in k else 4))
        _DTS[k] = v
    return v


def _box(ap):
    t = ap.tensor
    sp = str(ap.space)
    if "DRAM" in sp.upper() or "HBM" in sp.upper():
        return None
    a = ap.ap
    pstride = a[0][0]
    off = int(ap.offset)
    if pstride == 0:
        p0, f0 = 0, off
        npart = 1
    else:
        p0, f0 = off // pstride, off % pstride
        npart = a[0][1]
    ext = 0
    for st, cnt in a[1:]:
        ext += abs(st) * (cnt - 1)
    ds = _dsize(ap.dtype)
    return (t.name, p0, p0 + npart, f0 * ds, (f0 + ext + 1) * ds)


def _ov(a, b):
    return a[0] == b[0] and a[1] < b[2] and b[1] < a[2] and a[3] < b[4] and b[3] < a[4]


def _cov(a, b):
    return a[0] == b[0] and a[1] <= b[1] and a[2] >= b[2] and a[3] <= b[3] and a[4] >= b[4]


class Op:
    __slots__ = ("eng", "fn", "deps", "idx", "tick", "signal", "waits", "dma", "dsem", "dval", "vc")

    def __init__(self, eng, fn):
        self.eng = eng
        self.fn = fn
        self.deps = set()
        self.tick = 0
        self.signal = False
        self.waits = []
        self.dma = False
        self.dsem = None
        self.dval = 0
        self.vc = None


class Sched:
    def __init__(self, nc, n_dma_sems=8):
        self.nc = nc
        self.ops = []
        self.writes = {}
        self.reads = {}
        self.n_dma_sems = n_dma_sems
        self.dma_count = {}
        self.es = ExitStack()

    def sb(self, name, shape, dtype):
        return self.es.enter_context(self.nc.sbuf_tensor(name, list(shape), dtype))

    def ps(self, name, shape, dtype=F32):
        return self.es.enter_context(self.nc.psum_tensor(name, list(shape), dtype))

    def op(self, eng, fn, reads=(), writes=(), dma=False):
        o = Op(eng, fn)
        o.idx = len(self.ops)
        o.dma = dma
        for ap in reads:
            b = _box(ap)
            if b is None:
                continue
            for (wb, wi) in self.writes.get(b[0], ()):
                if _ov(wb, b):
                    o.deps.add(wi)
        for ap in writes:
            b = _box(ap)
            if b is None:
                continue
            wl = self.writes.get(b[0], [])
            rl = self.reads.get(b[0], [])
            for (wb, wi) in wl:
                if _ov(wb, b):
                    o.deps.add(wi)
            for (rb, ri) in rl:
                if _ov(rb, b):
                    o.deps.add(ri)
            self.writes[b[0]] = [(wb, wi) for (wb, wi) in wl if not _cov(b, wb)]
            self.reads[b[0]] = [(rb, ri) for (rb, ri) in rl if not _cov(b, rb)]
        for ap in reads:
            b = _box(ap)
            if b is None:
                continue
            rl = self.reads.setdefault(b[0], [])
            if not dma:
                rl[:] = [(rb, ri) for (rb, ri) in rl
                         if ri == o.idx or not (_cov(b, rb) and self.ops[ri].eng == eng and not self.ops[ri].dma)]
            rl.append((b, o.idx))
        for ap in writes:
            b = _box(ap)
            if b is None:
                continue
            self.writes.setdefault(b[0], []).append((b, o.idx))
        o.deps.discard(o.idx)
        self.ops.append(o)
        return o

    def finalize(self):
        ops = self.ops
        need = {}
        for o in ops:
            nd = set()
            best = {}
            for d in o.deps:
                p = ops[d]
                if p.dma:
                    nd.add(d)
                    continue
                if p.eng != o.eng:
                    ok = True
                else:
                    ok = (o.eng in ("act", "dve", "pool") and not o.dma) or (o.dma and not p.dma)
                if ok and best.get(p.eng, -1) < d:
                    best[p.eng] = d
            nd.update(best.values())
            need[o.idx] = nd
            for d in nd:
                ops[d].signal = True
        cnt = {e: 0 for e in ENGS}
        for o in ops:
            if o.dma:
                continue
            if o.signal:
                cnt[o.eng] += 1
                o.tick = cnt[o.eng]
        dcount = {}
        self.dma_sem_use = {}
        for o in ops:
            if o.dma:
                k = dcount.get(o.eng, 0)
                dcount[o.eng] = k + 1
                slot = (o.eng, k % self.n_dma_sems)
                u = self.dma_sem_use.get(slot, 0) + 1
                self.dma_sem_use[slot] = u
                o.dsem = slot
                o.dval = 16 * u
        known = {e: {} for e in ENGS}
        for o in ops:
            kn = known[o.eng]
            waits = []
            if o.dma and o.dval > 16:
                key = o.dsem
                if kn.get(key, 0) < o.dval - 16:
                    waits.append((key, o.dval - 16))
                    kn[key] = o.dval - 16
            for d in sorted(need[o.idx]):
                p = ops[d]
                if p.dma:
                    key, val = p.dsem, p.dval
                else:
                    key, val = p.eng, p.tick
                if kn.get(key, 0) >= val:
                    continue
                waits.append((key, val))
                kn[key] = val
                if not p.dma and p.vc is not None:
                    for k2, v2 in p.vc.items():
                        if kn.get(k2, 0) < v2:
                            kn[k2] = v2
            wm = {}
            for k_, v_ in waits:
                if wm.get(k_, 0) < v_:
                    wm[k_] = v_
            o.waits = list(wm.items())
            if o.signal and not o.dma:
                o.vc = dict(kn)
                kn[o.eng] = max(kn.get(o.eng, 0), 0)

    def emit(self, final_waits_engine="sp"):
        nc = self.nc
        ops = self.ops
        sems = {}
        es = self.es
        for e in ENGS:
            sems[e] = es.enter_context(nc.semaphore("s_" + e))
        for slot in self.dma_sem_use:
            sems[slot] = es.enter_context(nc.semaphore("d_%s_%d" % slot))
        per = {e: [o for o in ops if o.eng == e] for e in ENGS}
        finals = []
        for slot, u in self.dma_sem_use.items():
            finals.append((slot, 16 * u))
        cnt = {e: max([o.tick for o in per[e] if not o.dma] + [0]) for e in ENGS}
        for e in ENGS:
            if cnt[e] > 0:
                finals.append((e, cnt[e]))

        def run(engname, eng):
            for o in per[engname]:
                for key, val in o.waits:
                    eng.wait_ge(sems[key], val)
                ins = o.fn(eng)
                if o.dma:
                    ins.then_inc(sems[o.dsem], 16)
                elif o.signal:
                    ins.then_inc(sems[o.eng], 1)
            if engname == final_waits_engine:
                for key, val in finals:
                    eng.wait_ge(sems[key], val)

        with nc.Block() as block:
            @block.tensor
            def _(e):
                run("pe", e)

            @block.scalar
            def _(e):
                run("act", e)

            @block.vector
            def _(e):
                run("dve", e)

            @block.gpsimd
            def _(e):
                run("pool", e)

            @block.sync
            def _(e):
                run("sp", e)

    def dma(self, out, in_, eng="sp", **kw):
        return self.op(eng, lambda e: e.dma_start(out=out, in_=in_, **kw), reads=[in_], writes=[out], dma=True)

    def mm(self, out, lhsT, rhs, start=True, stop=True, **kw):
        return self.op("pe", lambda e: e.matmul(out, lhsT, rhs, start=start, stop=stop, **kw),
                       reads=[lhsT, rhs] + ([] if start else [out]), writes=[out])

    def tr(self, out, in_, ident, **kw):
        return self.op("pe", lambda e: e.transpose(out, in_, ident, **kw), reads=[in_, ident], writes=[out])

    def act(self, out, in_, func, bias=None, scale=None, accum_out=None, eng="act"):
        kw = {}
        rd = [in_]
        wr = [out]
        if bias is not None:
            kw["bias"] = bias
            if not isinstance(bias, (int, float)):
                rd.append(bias)
        if scale is not None:
            kw["scale"] = scale
            if not isinstance(scale, (int, float)):
                rd.append(scale)
        if accum_out is not None:
            kw["accum_out"] = accum_out
            wr.append(accum_out)
        return self.op(eng, lambda e: e.activation(out, in_, func, **kw), reads=rd, writes=wr)

    def tt(self, out, in0, in1, op, eng="dve"):
        return self.op(eng, lambda e: e.tensor_tensor(out, in0, in1, op), reads=[in0, in1], writes=[out])

    def ts(self, out, in0, s1, s2, op0, op1=None, eng="dve", accum_out=None):
        rd = [in0]
        for s in (s1, s2):
            if s is not None and not isinstance(s, (int, float)):
                rd.append(s)
        wr = [out]
        kw = {}
        if accum_out is not None:
            wr.append(accum_out)
            kw["accum_out"] = accum_out
        if op1 is None:
            return self.op(eng, lambda e: e.tensor_scalar(out, in0, s1, None, op0, **kw), reads=rd, writes=wr)
        return self.op(eng, lambda e: e.tensor_scalar(out, in0, s1, s2, op0, op1, **kw), reads=rd, writes=wr)

    def stt(self, out, in0, scalar, in1, op0, op1, eng="dve"):
        rd = [in0, in1]
        if not isinstance(scalar, (int, float)):
            rd.append(scalar)
        return self.op(eng, lambda e: e.scalar_tensor_tensor(out, in0, scalar, in1, op0, op1), reads=rd, writes=[out])

    def copy(self, out, in_, eng="dve"):
        if eng == "act":
            return self.op(eng, lambda e: e.copy(out, in_), reads=[in_], writes=[out])
        return self.op(eng, lambda e: e.tensor_copy(out, in_), reads=[in_], writes=[out])

    def memset(self, out, val, eng="dve"):
        return self.op(eng, lambda e: e.memset(out, val), reads=[], writes=[out])

    def scan(self, out, d0, d1, initial, op0, op1, eng="dve"):
        rd = [d0, d1]
        if not isinstance(initial, (int, float)):
            rd.append(initial)
        return self.op(eng, lambda e: e.tensor_tensor_scan(out, d0, d1, initial, op0, op1), reads=rd, writes=[out])

    def recip(self, out, in_, eng="dve"):
        return self.op(eng, lambda e: e.reciprocal(out, in_), reads=[in_], writes=[out])

    def reduce(self, out, in_, op, axis=AX.X, eng="dve"):
        return self.op(eng, lambda e: e.tensor_reduce(out, in_, axis, op), reads=[in_], writes=[out])


D_MODEL = 1024
SEQ = 2048
MEM_LEN = 256
EPS = 1e-6
TWO_PI = 6.283185307179586
AP_ = bass.AP


def mk(base, dims):
    return AP_(base.tensor, base.offset, [list(base.ap[0])] + [list(d) for d in dims])


class Arena:
    def __init__(self, S, nbytes):
        self.t = S.sb("arena", [128, nbytes // 4], F32)
        self.free = [[0, nbytes, 0]]
        self.live = {}
        self.k = 0
        self.clock = 1

    def _view(self, o, nb, shape, dtype, n):
        v = self.t[:, o // 4:(o + nb) // 4]
        if dtype != F32:
            v = v.bitcast(dtype)
        v = v[:, 0:n]
        if len(shape) == 2:
            v = v.rearrange("p (a b) -> p a b", a=shape[0])
        elif len(shape) == 3:
            v = v.rearrange("p (a b c) -> p a b c", a=shape[0], b=shape[1])
        elif len(shape) == 4:
            v = v.rearrange("p (a b c d) -> p a b c d", a=shape[0], b=shape[1], c=shape[2])
        return v

    def alloc(self, shape, dtype, persistent=False):
        n = 1
        for d in shape:
            n *= d
        nb = n * _dsize(dtype)
        nb = (nb + 63) // 64 * 64
        cands = [b for b in self.free if b[1] >= nb]
        if not cands:
            raise RuntimeError("arena OOM need %d free %s" % (nb, self.free))
        if persistent or nb >= 16384:
            blk = min(cands, key=lambda b: b[0])
        else:
            blk = min(cands, key=lambda b: (b[2], b[0]))
        o = blk[0]
        if blk[1] == nb:
            self.free.remove(blk)
        else:
            blk[0] += nb
            blk[1] -= nb
        self.k += 1
        self.live[self.k] = (o, nb)
        return self._view(o, nb, shape, dtype, n), self.k

    def release(self, tok):
        o, nb = self.live.pop(tok)
        self.clock += 1
        self.free.append([o, nb, self.clock])
        self.free.sort()
        m = []
        for b in self.free:
            if m and m[-1][0] + m[-1][1] == b[0]:
                m[-1][1] += b[1]
                m[-1][2] = max(m[-1][2], b[2])
            else:
                m.append(b)
        self.free = m


class Ctx:
    pass


def build(layers=(0, 1), dbg=()):
    nc = bass.Bass("TRN2", target_bir_lowering=False)
    S = Sched(nc, n_dma_sems=8)
    Dm = {}

    def din(name, shape, dt=F32):
        Dm[name] = nc.dram_tensor(name, list(shape), dt, kind="ExternalInput").ap()
        return Dm[name]

    x_d = din("x", [SEQ, D_MODEL])
    mem_d = din("mem", [MEM_LEN, D_MODEL])
    pos_d = din("pos", [1, SEQ], I32)
    out_d = nc.dram_tensor("out", [SEQ, D_MODEL], F32, kind="ExternalOutput").ap()
    dbg_d = {}

    A = Arena(S, 212000 // 64 * 64)
    PS = S.ps("psum", [128, 4096], F32)

    def bank(b, n=512, off=0):
        return PS[:, 512 * b + off:512 * b + off + n]

    def bank_bf(b):
        return PS[:, 512 * b:512 * (b + 1)].bitcast(BF16)

    C = Ctx()
    C.dbg = dbg
    C.dbg_d = dbg_d

    def dbg_out(name, ap_sb, shape):
        d = nc.dram_tensor("dbg_" + name, list(shape), F32, kind="ExternalOutput").ap()
        dbg_d[name] = d
        S.dma(d, ap_sb)
    C.dbg_out = dbg_out
    C.nc, C.S, C.A, C.PS, C.bank, C.bank_bf, C.din, C.D = nc, S, A, PS, bank, bank_bf, din, Dm

    ident_d = din("c_ident", [128, 128])
    idf, t_idf = A.alloc([128], F32)
    S.dma(idf, ident_d)
    C.ident, _ = A.alloc([128], BF16)
    S.copy(C.ident, idf)
    C.identf = idf
    C.ones, _ = A.alloc([128], BF16)
    S.memset(C.ones, 1.0)

    class WPool:
        def __init__(self, n_st, n_bf, elems):
            self.bf = [A.alloc([elems], BF16) for _ in range(n_bf)]
            self.c = 0

        def free(self):
            for _, k in self.bf:
                A.release(k)

        def load(self, W, kt_n, c0, ncols, dst=None):
            Wv = W.rearrange("(k p) c -> p k c", p=128)
            if dst is None:
                wb, _ = self.bf[self.c % len(self.bf)]
                self.c += 1
                dv = wb[:, 0:kt_n * ncols].rearrange("p (k c) -> p k c", k=kt_n)
            else:
                dv = dst
            h = max(1, kt_n // 2)
            S.dma(dv[:, 0:h, :], Wv[:, 0:h, c0:c0 + ncols], eng="pool")
            if h < kt_n:
                S.dma(dv[:, h:, :], Wv[:, h:, c0:c0 + ncols], eng="pool")
            return dv
    C.WPool = WPool

    def gain_row(g_d, l):
        g, k = A.alloc([1024], F32)
        S.dma(g, g_d[l:l + 1, :].partition_broadcast(128))
        return g, k
    C.gain_row = gain_row

    def norm_T(src_d, ntt, dstT, gbc):
        for tt in range(ntt):
            xt, t1 = A.alloc([1024], F32)
            S.dma(xt, src_d[tt * 128:(tt + 1) * 128, :])
            norm_T_tile(xt, tt, dstT, gbc, last=(tt == ntt - 1))
            A.release(t1)

    pend_evac = []

    def norm_T_tile(xt, tt, dstT, gbc, last=True):
        junk, t2 = A.alloc([1024], BF16)
        ss, t3 = A.alloc([2], F32)
        S.memset(ss, 0.0)
        S.act(junk, xt, AF.Square, accum_out=ss[:, 0:1])
        C.rstd(ss[:, 0:1], ss[:, 0:1], 1.0 / D_MODEL, tmp=ss[:, 1:2])
        xn, t4 = A.alloc([1024], BF16)
        S.stt(xn, xt, ss[:, 0:1], gbc, ALU.mult, ALU.mult)
        pb = C.bank_bf(C.trb[0] % 2)
        C.trb[0] += 1
        for k in range(8):
            S.tr(pb[:, k * 128:(k + 1) * 128], xn[:, k * 128:(k + 1) * 128], C.ident)
        while pend_evac:
            pend_evac.pop(0)()
        pend_evac.append(lambda: S.copy(dstT[:, :, tt * 128:(tt + 1) * 128], pb.rearrange("p (k c) -> p k c", k=8), eng="dve"))
        if last:
            while pend_evac:
                pend_evac.pop(0)()
        for t in (t2, t3, t4):
            A.release(t)
    C.trb = [0]
    C.eps_col, _ = A.alloc([1], F32)
    S.memset(C.eps_col, EPS)
    C.one_col, _ = A.alloc([1], F32)
    S.memset(C.one_col, 1.0)

    def rstd(out, in_, inv_n, tmp=None):
        npart = out.shape[0]
        t = out if tmp is None else tmp
        S.act(t, in_, AF.Ln, bias=C.eps_col[0:npart], scale=inv_n)
        S.act(out, t, AF.Exp, scale=-0.5)
    C.rstd = rstd

    def sigmoid(out, in_):
        npart = out.shape[0]
        S.act(out, in_, AF.Exp, scale=-1.0)
        S.act(out, out, AF.Ln, bias=C.one_col[0:npart], scale=1.0)
        S.act(out, out, AF.Exp, scale=-1.0)
    C.sigmoid = sigmoid

    def silu_gate(dst, val, pg):
        sg, k1 = A.alloc([512], F32)
        sigmoid(sg, pg)
        S.tt(sg, sg, val, ALU.mult)
        S.tt(dst, sg, pg, ALU.mult)
        A.release(k1)
    C.silu_gate = silu_gate
    C.norm_T, C.norm_T_tile = norm_T, norm_T_tile

    C.lng_d = din("ln_gain", [2, 1024])
    C.mng_d = din("mem_norm", [2, 1024])
    C.mem_d = mem_d
    xqn_d = din("p_xq_norm", [128, 2])
    xkn_d = din("p_xk_norm", [128, 2])
    C.xqn, _ = A.alloc([2], F32, persistent=True)
    C.xkn, _ = A.alloc([2], F32, persistent=True)
    S.dma(C.xqn, xqn_d)
    S.dma(C.xkn, xkn_d)
    wout_d = din("w_out", [2, 2048, 1024])
    wmem_d = din("w_mem_kv", [2, 1024, 1024])
    C.wout_d, C.wmem_d = wout_d, wmem_d


    def fnorm(src_ps, n, gain_col, dst, psb, ndim=128):
        sq, t1 = A.alloc([512], BF16)
        S.act(sq[0:ndim, 0:n], src_ps, AF.Square)
        S.mm(psb[0:ndim, 0:n], C.ones[0:ndim, 0:ndim], sq[0:ndim, 0:n])
        rs, t2 = A.alloc([512], F32)
        C.rstd(rs[0:ndim, 0:n], psb[0:ndim, 0:n], 1.0 / ndim)
        S.stt(dst, src_ps, gain_col, rs[0:ndim, 0:n], ALU.mult, ALU.mult)
        A.release(t1)
        A.release(t2)
    C.fnorm = fnorm

    def mem_kv(l, P, wk_pre=None, wv_pre=None):
        KmT, tk = A.alloc([4, MEM_LEN], BF16)
        Vm, tv = A.alloc([2, 512], BF16)
        memnT, k_mn = A.alloc([8, MEM_LEN], BF16)
        gm, k_gm = C.gain_row(C.mng_d, l)
        norm_T(C.mem_d, 2, memnT, gm)
        A.release(k_gm)
        wk = wk_pre if wk_pre is not None else P.load(wmem_d[l], 8, 0, 512)
        for h in range(4):
            pk = bank(h % 2, 256)
            for k in range(8):
                S.mm(pk, wk[:, k, h * 128:(h + 1) * 128], memnT[:, k, :], start=(k == 0), stop=(k == 7))
            fnorm(pk, 256, C.xkn[:, l:l + 1], KmT[:, h, :], bank(2 + h % 2, 256))
        wv = wv_pre if wv_pre is not None else P.load(wmem_d[l], 8, 512, 512)
        for kt in range(2):
            pv = bank(4 + kt)
            for k in range(8):
                S.mm(pv, memnT[:, k, kt * 128:(kt + 1) * 128], wv[:, k, :], start=(k == 0), stop=(k == 7))
            S.copy(Vm[:, kt, :], pv, eng="act")
        A.release(k_mn)
        return KmT, Vm, tk, tv
    C.mem_kv = mem_kv

    def mem_attn_head(l, h, xq_ps_fn, hT, gate_fn, oT, KmT, Vm):
        pass

    if 0 in layers:
        layer_s5(C, x_d)
    else:
        C.x1, _ = A.alloc([16, 1024], F32)
        for tt in range(16):
            S.dma(C.x1[:, tt, :], x_d[tt * 128:(tt + 1) * 128, :])
    if 1 in layers:
        layer_mla(C, pos_d)
    for tt in range(16):
        S.dma(out_d[tt * 128:(tt + 1) * 128, :], C.x1[:, tt, :])
    S.finalize()
    S.emit()
    S.es.close()
    return nc


def sincos(C, t, n, s_out, c_out, pre_scaled=False, eng="dve"):
    S, A = C.S, C.A
    r, t1 = A.alloc([n], F32)
    t2 = None
    if eng == "pool":
        MAGIC = 12582912.0
        rf, t3 = A.alloc([n], F32)
        if pre_scaled:
            S.ts(rf, t, MAGIC, None, ALU.add, eng=eng)
            src = t
        else:
            S.ts(r, t, 1.0 / TWO_PI, None, ALU.mult, eng=eng)
            S.ts(rf, r, MAGIC, None, ALU.add, eng=eng)
            src = r
        S.ts(rf, rf, -MAGIC, None, ALU.add, eng=eng)
        S.tt(r, src, rf, ALU.subtract, eng=eng)
        A.release(t3)
    else:
        ri, t2 = A.alloc([n], I32)
        if pre_scaled:
            S.copy(ri, t)
            S.copy(r, ri)
            S.tt(r, t, r, ALU.subtract)
        else:
            S.ts(r, t, 1.0 / TWO_PI, None, ALU.mult)
            S.copy(ri, r)
            rf, t3 = A.alloc([n], F32)
            S.copy(rf, ri)
            S.tt(r, r, rf, ALU.subtract)
            A.release(t3)
    S.act(s_out, r, AF.Sin, scale=6.283185)
    h, t4 = A.alloc([n], F32)
    S.act(h, r, AF.Sin, scale=3.1415925)
    S.act(h, h, AF.Square)
    S.act(c_out, h, AF.Identity, bias=C.one_col, scale=-2.0)
    for t_ in (t1, t2, t4):
        if t_ is not None:
            A.release(t_)


def cmul(C, or_, oi_, ar, ai, br, bi, n_shape, eng="dve", neg_i=False):
    S, A = C.S, C.A
    t1, k1 = A.alloc(n_shape, F32)
    t2, k2 = A.alloc(n_shape, F32)
    S.tt(t1, ar, br, ALU.mult, eng=eng)
    S.tt(t2, ai, bi, ALU.mult, eng=eng)
    S.tt(or_, t1, t2, ALU.subtract, eng=eng)
    S.tt(t1, ar, bi, ALU.mult, eng=eng)
    S.tt(t2, ai, br, ALU.mult, eng=eng)
    S.tt(oi_, t1, t2, ALU.add, eng=eng)
    A.release(k1)
    A.release(k2)


def layer_s5(C, x_d):
    S, A, nc, bank, din = C.S, C.A, C.nc, C.bank, C.din
    l = 0
    snap = set(A.live)
    w_in = din("s5_w_in", [1024, 4096])
    w_glu = din("s5_w_glu", [1536, 3072])
    names_L = ["lamr_L", "lami_L", "lstep_L"]
    pL = {}
    tokL, tokR = {}, {}
    for nm in names_L:
        d = din("s5_" + nm, [128, 48])
        pL[nm], tokL[nm] = A.alloc([48], F32)
        S.dma(pL[nm], d)
    for nm in ["B_r_L", "B_i_L", "C_r_L", "C_i_L"]:
        d = din("s5_" + nm, [128, 48, 16])
        pL[nm], tokL[nm] = A.alloc([48, 16], F32)
        S.dma(pL[nm], d)
    pR = {}
    for nm in ["lamr_R", "lami_R", "lstep_R", "B_r_R", "B_i_R"]:
        d = din("s5_" + nm, [128, 12, 64])
        pR[nm], tokR[nm] = A.alloc([12, 64], F32)
        S.dma(pR[nm], d)
    d_d = din("s5_d_col", [128, 12])
    dcol, _ = A.alloc([12], F32)
    S.dma(dcol, d_d)
    cst = {}
    for nm, shp in [("c_maskL", [2]), ("c_maskLn", [2]), ("c_maskR", [2]), ("c_sv", [16]), ("c_jiota", [256])]:
        d = din(nm, [128] + shp)
        cst[nm], _ = A.alloc(shp, F32)
        S.dma(cst[nm], d)
    def setup(P, n, lamr, lami, lstep, eng="dve"):
        delta, k0 = A.alloc([n], F32)
        S.act(delta, lstep, AF.Exp)
        zr, _ = A.alloc([n], F32)
        zi, _ = A.alloc([n], F32)
        S.tt(zr, lamr, delta, ALU.mult, eng=eng)
        S.tt(zi, lami, delta, ALU.mult, eng=eng)
        A.release(k0)
        mag, k1 = A.alloc([n], F32)
        S.act(mag, zr, AF.Exp)
        sn, k2 = A.alloc([n], F32)
        cs, k3 = A.alloc([n], F32)
        sincos(C, zi, n, sn, cs, eng=eng)
        am1, k4 = A.alloc([n], F32)
        ai, k5 = A.alloc([n], F32)
        S.tt(am1, mag, cs, ALU.mult, eng=eng)
        S.ts(am1, am1, -1.0, None, ALU.add, eng=eng)
        S.tt(ai, mag, sn, ALU.mult, eng=eng)
        n2, k6 = A.alloc([n], F32)
        t, k7 = A.alloc([n], F32)
        S.tt(n2, lamr, lamr, ALU.mult, eng=eng)
        S.tt(t, lami, lami, ALU.mult, eng=eng)
        S.tt(n2, n2, t, ALU.add, eng=eng)
        S.recip(n2, n2)
        bf_r, kbr = A.alloc([n], F32)
        bf_i, kbi = A.alloc([n], F32)
        t2, k8 = A.alloc([n], F32)
        S.tt(t, am1, lamr, ALU.mult, eng=eng)
        S.tt(t2, ai, lami, ALU.mult, eng=eng)
        S.tt(t, t, t2, ALU.add, eng=eng)
        S.tt(bf_r, t, n2, ALU.mult, eng=eng)
        S.tt(t, ai, lamr, ALU.mult, eng=eng)
        S.tt(t2, am1, lami, ALU.mult, eng=eng)
        S.tt(t, t, t2, ALU.subtract, eng=eng)
        S.tt(bf_i, t, n2, ALU.mult, eng=eng)
        for k in (k1, k2, k3, k4, k5, k6, k7, k8):
            A.release(k)
        return zr, zi, bf_r, bf_i, kbr, kbi

    zrL, ziL, bfrL, bfiL, kbrL, kbiL = setup(pL, 48, pL["lamr_L"], pL["lami_L"], pL["lstep_L"])
    sv = cst["c_sv"]
    angS, k0 = A.alloc([48, 8], F32)
    magS, k1 = A.alloc([48, 8], F32)
    zi_b = mk(ziL, [(1, 48), (0, 8)])
    zr_b = mk(zrL, [(1, 48), (0, 8)])
    sv_b = mk(sv, [(0, 48), (1, 8)])
    S.tt(angS, zi_b, sv_b, ALU.mult)
    S.tt(magS, zr_b, sv_b, ALU.mult)
    S.act(magS, magS, AF.Exp)
    snS, k2 = A.alloc([48, 8], F32)
    csS, k3 = A.alloc([48, 8], F32)
    fl = lambda a: a.rearrange("p a b -> p (a b)")
    sincos(C, fl(angS), 384, fl(snS), fl(csS))
    powr, _ = A.alloc([48, 8], F32)
    powi, _ = A.alloc([48, 8], F32)
    S.tt(powr, magS, csS, ALU.mult)
    S.tt(powi, magS, snS, ALU.mult)
    for k in (k0, k1, k2, k3):
        A.release(k)
    rho, _ = A.alloc([48], F32)
    S.act(rho, zrL, AF.Exp, scale=8.0)
    phir, _ = A.alloc([48], F32)
    t, k0 = A.alloc([48], F32)
    ti, k1 = A.alloc([48], I32)
    S.ts(t, ziL, 8.0 / TWO_PI, None, ALU.mult)
    S.copy(ti, t)
    S.copy(phir, ti)
    S.tt(phir, t, phir, ALU.subtract)
    A.release(k0)
    A.release(k1)
    BbrL, kb1 = A.alloc([48, 16], F32)
    BbiL, kb2 = A.alloc([48, 16], F32)
    bfr_b = mk(bfrL, [(1, 48), (0, 16)])
    bfi_b = mk(bfiL, [(1, 48), (0, 16)])
    cmul(C, BbrL, BbiL, bfr_b, bfi_b, pL["B_r_L"], pL["B_i_L"], [48, 16])
    BbZ, _ = A.alloc([48, 2, 2, 16], BF16)
    mL = cst["c_maskL"]
    mLn = cst["c_maskLn"]
    for ri, src in ((0, BbrL), (1, BbiL)):
        S.tt(BbZ[:, :, ri, :, :], mk(src, [(16, 48), (0, 2), (1, 16)]), mk(mL, [(0, 48), (1, 2), (0, 16)]), ALU.mult)
    A.release(kb1)
    A.release(kb2)
    for nm in ["B_r_L", "B_i_L", "lamr_L", "lami_L", "lstep_L"]:
        A.release(tokL[nm])
    A.release(kbrL)
    A.release(kbiL)

    fl2 = lambda a: a.rearrange("p a b -> p (a b)")
    zrR, ziR, bfrR, bfiR, kbrR, kbiR = setup(pR, 768, fl2(pR["lamr_R"]), fl2(pR["lami_R"]), fl2(pR["lstep_R"]), eng="pool")
    BbrR, _ = A.alloc([12, 64], F32)
    BbiR, _ = A.alloc([12, 64], F32)
    cmul(C, fl2(BbrR), fl2(BbiR), bfrR, bfiR, fl2(pR["B_r_R"]), fl2(pR["B_i_R"]), [768], eng="pool")
    for nm in tokR:
        A.release(tokR[nm])
    A.release(kbrR)
    A.release(kbiR)
    zrR3 = zrR.rearrange("p (a b) -> p a b", a=12)
    ziR3 = ziR.rearrange("p (a b) -> p a b", a=12)

    hT, k_hT = A.alloc([8, SEQ], BF16)
    ygT, k_yg = A.alloc([12, SEQ], BF16)
    g_ln, k_gln = C.gain_row(C.lng_d, l)
    C.norm_T(x_d, 16, hT, g_ln)
    A.release(k_gln)

    PA = C.WPool(1, 2, 2048)
    S1 = [A.alloc([8, 128], BF16)[0] for _ in range(3)]
    for s1 in S1:
        S.memset(s1, 0.0)
    S2 = [A.alloc([8, 2, 2, 64], BF16)[0] for _ in range(2)]
    S3 = [A.alloc([4, 8, 2, 32], BF16)[0] for _ in range(3)]
    uTs = [A.alloc([SEQ], BF16)[0] for _ in range(2)]
    Fb = [[A.alloc([4, 256], BF16)[0] for _ in range(2)] for _ in range(2)]
    fbuf = [A.alloc([4, 264], F32)[0] for _ in range(2)]
    for fb in fbuf:
        S.memset(fb, 0.0)
    jio = cst["c_jiota"]
    mR = cst["c_maskR"]

    wb_cur = [None]

    TE = "pool"

    def tables(i):
        par = i % 2
        ctr, k1 = A.alloc([4, 8, 16], F32)
        cti, k2 = A.alloc([4, 8, 16], F32)
        q0 = 4 * i
        pr_b = mk(powr[:, q0:q0 + 4, :], [(8, 4), (1, 8), (0, 16)])
        pi_b = mk(powi[:, q0:q0 + 4, :], [(8, 4), (1, 8), (0, 16)])
        cr_b = mk(pL["C_r_L"][:, q0:q0 + 4, :], [(16, 4), (0, 8), (1, 16)])
        ci_b = mk(pL["C_i_L"][:, q0:q0 + 4, :], [(16, 4), (0, 8), (1, 16)])
        cmul(C, ctr, cti, pr_b, pi_b, cr_b, ci_b, [4, 8, 16], eng=TE)
        s3 = S3[i % 3]
        s3v = s3.rearrange("p q s r c -> p (q s) r c")
        for ri, src, msk in ((0, ctr, mL), (1, cti, mLn)):
            S.tt(mk(s3v[:, :, ri, :], [(64, 32), (16, 2), (1, 16)]),
                 mk(src, [(16, 32), (0, 2), (1, 16)]),
                 mk(msk, [(0, 32), (1, 2), (0, 16)]), ALU.mult, eng=TE)
        A.release(k1)
        A.release(k2)
        ang, k1 = A.alloc([8, 64], F32)
        mag, k2 = A.alloc([8, 64], F32)
        svn_b = mk(sv[:, 8:16], [(1, 8), (0, 64)])
        S.tt(ang, mk(ziR3[:, i, :], [(0, 8), (1, 64)]), svn_b, ALU.mult, eng="dve")
        S.tt(mag, mk(zrR3[:, i, :], [(0, 8), (1, 64)]), svn_b, ALU.mult, eng="dve")
        S.act(mag, mag, AF.Exp)
        sn, k3 = A.alloc([8, 64], F32)
        cs, k4 = A.alloc([8, 64], F32)
        sincos(C, fl(ang), 512, fl(sn), fl(cs), eng="dve")
        S.tt(cs, cs, mag, ALU.mult, eng="dve")
        S.tt(sn, sn, mag, ALU.mult, eng="dve")
        btr, k5 = A.alloc([8, 64], F32)
        bti, k6 = A.alloc([8, 64], F32)
        cmul(C, btr, bti, cs, sn, mk(BbrR[:, i, :], [(0, 8), (1, 64)]), mk(BbiR[:, i, :], [(0, 8), (1, 64)]), [8, 64], eng="dve")
        s2 = S2[par]
        for ri, src in ((0, btr), (1, bti)):
            S.tt(s2[:, :, ri, :, :], mk(src, [(64, 8), (0, 2), (1, 64)]), mk(mR, [(0, 8), (1, 2), (0, 64)]), ALU.mult, eng="dve")
        for k in (k1, k2, k3, k4, k5, k6):
            A.release(k)
    Dhi = [A.alloc([128], BF16)[0] for _ in range(3)]
    Dlo = [A.alloc([128], BF16)[0] for _ in range(3)]

    def dtables(i):
        t1, k1 = A.alloc([128], F32)
        t2, k2 = A.alloc([128], F32)
        S.ts(t1, C.identf, dcol[:, i:i + 1], None, ALU.mult, eng=TE)
        S.copy(Dhi[i % 3], t1, eng=TE)
        S.copy(t2, Dhi[i % 3], eng=TE)
        S.tt(t1, t1, t2, ALU.subtract, eng=TE)
        S.copy(Dlo[i % 3], t1, eng=TE)
        A.release(k1)
        A.release(k2)

    def s1gen(i):
        q0 = 4 * i
        s3 = S3[i % 3]
        s1 = S1[i % 3]
        for q in range(4):
            po = C.PS[32 * q:32 * q + 32, 0:256]
            kw = {"tile_position": (0, 96)} if q == 3 else {}
            po3 = po.rearrange("p (t c) -> p t c", t=8)
            S.mm(po3, BbZ[:, q0 + q, 0, :, :].rearrange("p a b -> p (a b)"), s3[:, q, :, 0, :], start=True, stop=False, **kw)
            S.mm(po3, BbZ[:, q0 + q, 1, :, :].rearrange("p a b -> p (a b)"), s3[:, q, :, 1, :], start=False, stop=True, **kw)
            S.copy(s1[32 * q:32 * q + 32, :, 32 * q:32 * q + 32], po3, eng="act")

    def uproj(i):
        par = i % 2
        if i % 2 == 0:
            wb_cur[0] = PA.load(w_in, 8, (i // 2) * 256, 256)
        wb = wb_cur[0]
        lc = (i % 2) * 128
        uT = uTs[par]
        for tb in range(4):
            pb = bank(tb % 2)
            for k in range(8):
                S.mm(pb, wb[:, k, lc:lc + 128], hT[:, k, tb * 512:(tb + 1) * 512], start=(k == 0), stop=(k == 7))
            S.copy(uT.rearrange("p (s j) -> p s j", s=8)[:, :, tb * 64:(tb + 1) * 64],
                   pb.rearrange("p (j s) -> p s j", s=8), eng="act")

    def xprime(i):
        par = i % 2
        uT, s2 = uTs[par], S2[par]
        u3 = uT.rearrange("p (s j) -> p s j", s=8)
        for q in range(4):
            kw = {"tile_position": (96, 0)} if q == 3 else {}
            for ri in range(2):
                po = C.PS[:, 1024 + ri * 1024 + q * 256: 1024 + ri * 1024 + (q + 1) * 256]
                for sp in range(8):
                    S.mm(po, s2[32 * q:32 * q + 32, sp, ri, :, :].rearrange("p a b -> p (a b)"),
                         u3[32 * q:32 * q + 32, sp, :], start=(sp == 0), stop=(sp == 7), **kw)

    rot_sn, _ = A.alloc([4, 256], F32)
    rot_cs, _ = A.alloc([4, 256], F32)

    def rot_tables(i):
        q0 = 4 * i
        r, k1 = A.alloc([4, 256], F32)
        for q in range(4):
            S.ts(r[:, q, :], jio, phir[:, q0 + q:q0 + q + 1], None, ALU.mult)
        sincos(C, fl(r), 1024, fl(rot_sn), fl(rot_cs), pre_scaled=True)
        A.release(k1)

    def chunk_scan(i):
        par = i % 2
        q0 = 4 * i
        fr, fi = fbuf
        Fr, Fi = Fb[par]
        sn, cs = rot_sn, rot_cs
        xr = C.PS[:, 1024:2048].rearrange("p (q j) -> p q j", q=4)
        xi = C.PS[:, 2048:3072].rearrange("p (q j) -> p q j", q=4)
        vr, k4 = A.alloc([4, 256], F32)
        vi, k5 = A.alloc([4, 256], F32)
        t, k6 = A.alloc([4, 256], F32)
        S.tt(vr, xr, cs, ALU.mult)
        S.tt(t, xi, sn, ALU.mult)
        S.tt(vi, xi, cs, ALU.mult)
        S.tt(vr, vr, t, ALU.add)
        S.tt(t, xr, sn, ALU.mult)
        S.tt(vi, vi, t, ALU.subtract)
        for q in range(4):
            rb = mk(rho[:, q0 + q:q0 + q + 1], [(0, 256)])
            S.scan(fr[:, q, 1:257], vr[:, q, :], rb, 0.0, ALU.add, ALU.mult)
            S.scan(fi[:, q, 1:257], vi[:, q, :], rb, 0.0, ALU.add, ALU.mult)
        frh = fr[:, :, 0:256]
        fih = fi[:, :, 0:256]
        S.tt(vr, frh, cs, ALU.mult)
        S.tt(t, fih, sn, ALU.mult)
        S.tt(Fr, vr, t, ALU.subtract)
        S.tt(vi, frh, sn, ALU.mult)
        S.tt(t, fih, cs, ALU.mult)
        S.tt(Fi, vi, t, ALU.add)
        for k in (k4, k5, k6):
            A.release(k)

    def youts(i):
        par = i % 2
        uT, s1, s3 = uTs[par], S1[i % 3], S3[i % 3]
        Fr, Fi = Fb[par]
        u3 = uT.rearrange("p (s j) -> p s j", s=8)
        yg3 = ygT[:, i, :].rearrange("p (j s) -> p s j", s=8)
        for sp in range(4):
            b = 6 + sp % 2
            for s in (2 * sp, 2 * sp + 1):
                po = bank(b, 256, (s % 2) * 256)
                for s_ in range(s + 1):
                    S.mm(po, s1[:, s - s_, :], u3[:, s_, :], start=(s_ == 0), stop=False)
                S.mm(po, Dhi[i % 3], u3[:, s, :], start=False, stop=False)
                S.mm(po, Dlo[i % 3], u3[:, s, :], start=False, stop=False)
                for q in range(4):
                    kw = {"tile_position": (0, 96)} if q == 3 else {}
                    pq = C.PS[32 * q:32 * q + 32, 512 * b + (s % 2) * 256: 512 * b + (s % 2) * 256 + 256]
                    S.mm(pq, s3[:, q, s, 0, :], Fr[:, q, :], start=False, stop=False, **kw)
                    S.mm(pq, s3[:, q, s, 1, :], Fi[:, q, :], start=False, stop=True, **kw)
            S.act(yg3[:, 2 * sp:2 * sp + 2, :], bank(b).rearrange("p (s j) -> p s j", s=2), AF.Gelu)

    def dump(name, ap, shape_free, src_dt=F32):
        n = 1
        for d_ in shape_free:
            n *= d_
        tmp, kk = A.alloc([n], F32)
        S.copy(tmp, ap if len(shape_free) == 1 else ap)
        C.dbg_out(name, tmp, [128, n])
        A.release(kk)

    tables(0)
    dtables(0)
    s1gen(0)
    tables(1)
    dtables(1)
    uproj(0)
    xprime(0)
    rot_tables(0)
    for i in range(12):
        if i == 0 and "t0" in C.dbg:
            dump("hT0", hT[:, 0, :], [SEQ])
            dump("u0", uTs[0], [SEQ])
            dump("S1", S1[0].rearrange("p a b -> p (a b)"), [1024])
            dump("S2", S2[0].rearrange("p a b c d -> p (a b c d)"), [2048])
            dump("S3", S3[0].rearrange("p a b c d -> p (a b c d)"), [2048])
            pass
            dump("powr", powr.rearrange("p a b -> p (a b)"), [384])
            dump("powi", powi.rearrange("p a b -> p (a b)"), [384])
            dump("rho", rho, [48])
            dump("phir", phir, [48])
            dump("bbz", BbZ.rearrange("p a b c d -> p (a b c d)"), [48 * 64])
        chunk_scan(i)
        if i + 1 < 12:
            uproj(i + 1)
            xprime(i + 1)
        if i + 2 < 12:
            tables(i + 2)
            dtables(i + 2)
        if i + 1 < 12:
            rot_tables(i + 1)
        youts(i)
        if i + 1 < 12:
            s1gen(i + 1)
    if "yg" in C.dbg:
        for i in range(12):
            tmp, kk = A.alloc([SEQ], F32)
            S.copy(tmp, ygT[:, i, :])
            C.dbg_out("yg%d" % i, tmp, [128, SEQ])
            A.release(kk)

    for k in list(A.live):
        if k not in snap and k not in (k_hT, k_yg):
            A.release(k)
    oT, k_oT = A.alloc([16, SEQ], BF16)
    gate_c0 = 2048

    def gate_mul(i, tb, val_f32, wg, lc):
        pg = bank(4 + (C.gcnt[0] % 2))
        C.gcnt[0] += 1
        for k in range(8):
            S.mm(pg, wg[:, k, lc:lc + 128], hT[:, k, tb * 512:(tb + 1) * 512], start=(k == 0), stop=(k == 7))
        C.silu_gate(oT[:, i, tb * 512:(tb + 1) * 512], val_f32, pg)
    C.gcnt = [0]

    PB = C.WPool(2, 6, 1536)
    for i in range(12):
        wa = PB.load(w_glu, 12, i * 128, 128)
        wb_ = PB.load(w_glu, 12, 1536 + i * 128, 128)
        wg = PB.load(w_in, 8, gate_c0 + i * 128, 128)
        if True:
            lc = 0
            for tb in range(4):
                pa = bank(0 + tb % 2)
                pb_ = bank(2 + tb % 2)
                for k in range(12):
                    S.mm(pa, wa[:, k, lc:lc + 128], ygT[:, k, tb * 512:(tb + 1) * 512], start=(k == 0), stop=(k == 11))
                for k in range(12):
                    S.mm(pb_, wb_[:, k, lc:lc + 128], ygT[:, k, tb * 512:(tb + 1) * 512], start=(k == 0), stop=(k == 11))
                sig, k1 = A.alloc([512], F32)
                C.sigmoid(sig, pb_)
                mix, k2 = A.alloc([512], F32)
                S.tt(mix, pa, sig, ALU.mult)
                gate_mul(i, tb, mix, wg, lc)
                A.release(k1)
                A.release(k2)
    A.release(k_yg)
    PB.free()

    wo_pre = A.alloc([16, 1024], BF16)
    P0 = C.WPool(2, 0, 64)
    for cb in range(8):
        P0.load(C.wout_d[l], 16, cb * 128, 128, dst=wo_pre[0][:, :, cb * 128:(cb + 1) * 128])
    mem_attention(C, l, w_in, 1536, gate_c0 + 1536, None, hT, oT, gate_mul)
    A.release(k_hT)
    C.x1, _ = A.alloc([16, 1024], F32)

    out_proj(C, l, oT, x_src_d=x_d, wo_pre=wo_pre)
    A.release(k_oT)


def mem_item(C, l, h, tb, wq, hT, KmT, Vm, wg, wg_c0, dst, bset):
    S, A, bank = C.S, C.A, C.bank
    b0, b1, b2, b3 = [bank(b) for b in bset]
    scale = 128.0 ** -0.5
    st = {}

    def s1():
        for k in range(8):
            S.mm(b0, wq[:, k, h * 128:(h + 1) * 128], hT[:, k, tb * 512:(tb + 1) * 512], start=(k == 0), stop=(k == 7))
        st["sq"], st["k_sq"] = A.alloc([512], BF16)
        S.act(st["sq"], b0, AF.Square)

    def s2():
        S.mm(b1, C.ones, st["sq"])
        A.release(st["k_sq"])
        rs, k2 = A.alloc([512], F32)
        C.rstd(rs, b1, 1.0 / 128)
        st["xqn"], st["k_xqn"] = A.alloc([512], BF16)
        S.stt(st["xqn"], b0, C.xqn[:, l:l + 1], rs, ALU.mult, ALU.mult)
        A.release(k2)

    def s3():
        for kt, bb in ((0, b0), (1, b1)):
            S.mm(bb, KmT[:, h, kt * 128:(kt + 1) * 128], st["xqn"])
            st["pT", kt], st["k_pT", kt] = A.alloc([512], BF16)
            S.act(st["pT", kt], bb, AF.Exp, scale=scale)
        A.release(st["k_xqn"])

    def s4():
        for kt in range(2):
            S.mm(b2, Vm[:, kt, h * 128:(h + 1) * 128], st["pT", kt], start=(kt == 0), stop=(kt == 1))
            S.mm(b3, C.ones, st["pT", kt], start=(kt == 0), stop=(kt == 1))
            A.release(st["k_pT", kt])
        st["rinv"], st["k_rinv"] = A.alloc([512], F32)
        st["mo"], st["k_mo"] = A.alloc([512], F32)
        S.act(st["rinv"], b3, AF.Ln)
        S.copy(st["mo"], b2, eng="act")
        S.act(st["rinv"], st["rinv"], AF.Exp, scale=-1.0)

    def s5():
        S.tt(st["mo"], st["mo"], st["rinv"], ALU.mult)
        A.release(st["k_rinv"])
        for k in range(8):
            S.mm(b0, wg[:, k, wg_c0:wg_c0 + 128], hT[:, k, tb * 512:(tb + 1) * 512], start=(k == 0), stop=(k == 7))
        C.silu_gate(dst, st["mo"], b0)
        A.release(st["k_mo"])
    return [s1, s2, s3, s4, s5]


def run_interleaved(items, k=2):
    items = list(items)
    live = []
    nxt = 0
    while live or nxt < len(items):
        while len(live) < k and nxt < len(items):
            live.append(list(items[nxt]))
            nxt += 1
        for it in list(live):
            it.pop(0)()
            if not it:
                live.remove(it)


def mem_attention(C, l, w_in, xq_c0, gate_c0, gain, hT, oT, gate_mul):
    S, A, bank = C.S, C.A, C.bank
    P = C.WPool(1, 4, 4096)
    KmT, Vm, tk, tv = C.mem_kv(l, P)
    wq = P.load(w_in, 8, xq_c0, 512)
    wg = P.load(w_in, 8, gate_c0, 512)
    items = []
    n = 0
    for h in range(4):
        for tb in range(4):
            bset = (0, 1, 2, 3) if n % 2 == 0 else (4, 5, 6, 7)
            items.append(mem_item(C, l, h, tb, wq, hT, KmT, Vm, wg, h * 128, oT[:, 12 + h, tb * 512:(tb + 1) * 512], bset))
            n += 1
    run_interleaved(items, 2)
    A.release(tk)
    A.release(tv)
    P.free()


def out_proj(C, l, oT, x_src_d=None, wo_pre=None):
    S, A, bank = C.S, C.A, C.bank
    P = C.WPool(2, 0, 64)
    if wo_pre is not None:
        wo, kwo = wo_pre
    else:
        wo, kwo = A.alloc([16, 1024], BF16)
        for cb in range(8):
            P.load(C.wout_d[l], 16, cb * 128, 128, dst=wo[:, :, cb * 128:(cb + 1) * 128])
    for tt in range(16):
        if x_src_d is not None:
            xt, kx = A.alloc([1024], F32)
            S.dma(xt, x_src_d[tt * 128:(tt + 1) * 128, :])
        else:
            xt = C.x1[:, tt, :]
        for dh in range(2):
            po = bank(dh + 2 * (tt % 2))
            for k in range(16):
                S.mm(po, oT[:, k, tt * 128:(tt + 1) * 128], wo[:, k, dh * 512:(dh + 1) * 512], start=(k == 0), stop=(k == 15))
            S.tt(C.x1[:, tt, dh * 512:(dh + 1) * 512], po, xt[:, dh * 512:(dh + 1) * 512], ALU.add)
        if x_src_d is not None:
            A.release(kx)
    A.release(kwo)
    P.free()


def out_proj_group(C, l, oG, kt0, nk, P):
    S, A, bank = C.S, C.A, C.bank
    wo = P.load(C.wout_d[l][kt0 * 128:(kt0 + nk) * 128, :], nk, 0, 1024)
    for tt in range(16):
        for dh in range(2):
            po = bank(dh)
            for k in range(nk):
                S.mm(po, oG[:, k, tt * 128:(tt + 1) * 128], wo[:, k, dh * 512:(dh + 1) * 512], start=(k == 0), stop=(k == nk - 1))
            xs = C.x1[:, tt, dh * 512:(dh + 1) * 512]
            S.tt(xs, po, xs, ALU.add)


def layer_mla(C, pos_d):
    S, A, nc, bank, din = C.S, C.A, C.nc, C.bank, C.din
    l = 1
    snap0 = set(A.live)
    w_in = din("mla_w_in_ext", [1024, 3584])
    w_uq = din("mla_w_uq_ext", [512, 3072])
    w_uk = din("mla_w_ukv_k", [256, 1536])
    w_uv = din("mla_w_ukv_v", [256, 1536])
    CQ0, CKV0, XQ0, G0, KR0, KRS0 = 0, 512, 768, 1280, 3328, 3456
    prm, prm_tok = {}, {}
    for nm, shp in [("qlg", [4]), ("kvlg", [2]), ("qng", [1]), ("kng", [1]), ("qrg", [2]), ("krg", [2]),
                    ("invf", [1]), ("sgn", [1]), ("tri", [128]), ("bones", [128])]:
        d = din("mla_" + nm, [128] + shp)
        prm[nm], prm_tok[nm] = A.alloc(shp, F32)
        S.dma(prm[nm], d)
    tri, _ = A.alloc([128], BF16)
    S.copy(tri, prm["tri"])
    bones, _ = A.alloc([128], BF16)
    S.copy(bones, prm["bones"])
    A.release(prm_tok["tri"])
    A.release(prm_tok["bones"])
    scale = 192.0 ** -0.5

    hT, k_hT = A.alloc([8, SEQ], BF16)
    P = C.WPool(1, 2, 4096)
    wk_pre = P.load(C.wmem_d[l], 8, 0, 512)
    wv_pre = P.load(C.wmem_d[l], 8, 512, 512)
    wq_buf, k_wq = A.alloc([8, 512], BF16)
    wq = P.load(w_in, 8, XQ0, 512, dst=wq_buf)
    g_ln, k_gln = C.gain_row(C.lng_d, l)
    for tt in range(16):
        C.norm_T_tile(C.x1[:, tt, :], tt, hT, g_ln, last=(tt == 15))
    A.release(k_gln)

    def proj_ps(pb, w, nk, c0, actT, tb):
        for k in range(nk):
            S.mm(pb, w[:, k, c0:c0 + 128], actT[:, k, tb * 512:(tb + 1) * 512], start=(k == 0), stop=(k == nk - 1))

    gcnt = [0]

    def gate_mul(dst, tb, val_f32, wg, lc):
        pg = bank(gcnt[0] % 2)
        gcnt[0] += 1
        proj_ps(pg, wg, 8, lc, hT, tb)
        C.silu_gate(dst, val_f32, pg)

    KmT, Vm, tk, tv = C.mem_kv(l, P, wk_pre, wv_pre)
    mscale = 128.0 ** -0.5
    oG, k_oG = A.alloc([4, SEQ], BF16)
    for hp in range(2):
        wg = P.load(w_in, 8, G0 + (12 + 2 * hp) * 128, 256)
        items = []
        n = 0
        for hh in range(2):
            h = 2 * hp + hh
            for tb in range(4):
                bset = (0, 1, 2, 3) if n % 2 == 0 else (4, 5, 6, 7)
                items.append(mem_item(C, l, h, tb, wq, hT, KmT, Vm, wg, hh * 128, oG[:, h, tb * 512:(tb + 1) * 512], bset))
                n += 1
        run_interleaved(items, 2)
    out_proj_group(C, l, oG, 12, 4, P)
    A.release(k_oG)
    A.release(tk)
    A.release(tv)
    A.release(k_wq)


    cqn, _ = A.alloc([4, SEQ], BF16)
    ckvn, _ = A.alloc([2, SEQ], BF16)

    def lora(c0, nt, dst, gcols):
        w = P.load(w_in, 8, c0, nt * 128)
        for tb in range(4):
            cf, k1 = A.alloc([nt, 512], F32)
            pss = bank(2)
            for m in range(nt):
                pb = bank(m % 2)
                proj_ps(pb, w, 8, m * 128, hT, tb)
                S.copy(cf[:, m, :], pb, eng="act")
                sq, k2 = A.alloc([512], BF16)
                S.act(sq, pb, AF.Square)
                S.mm(pss, C.ones, sq, start=(m == 0), stop=(m == nt - 1))
                A.release(k2)
            rs, k3 = A.alloc([512], F32)
            C.rstd(rs, pss, 1.0 / (nt * 128))
            for m in range(nt):
                S.stt(dst[:, m, tb * 512:(tb + 1) * 512], cf[:, m, :], gcols[:, m:m + 1], rs, ALU.mult, ALU.mult)
            A.release(k1)
            A.release(k3)
    lora(CQ0, 4, cqn, prm["qlg"])
    lora(CKV0, 2, ckvn, prm["kvlg"])

    cosT, _ = A.alloc([SEQ], F32)
    sinT, _ = A.alloc([SEQ], F32)
    posi, k1 = A.alloc([SEQ], I32)
    S.dma(posi, pos_d.partition_broadcast(128))
    ang, k2 = A.alloc([SEQ], F32)
    S.copy(ang, posi)
    S.ts(ang, ang, prm["invf"][:, 0:1], None, ALU.mult)
    A.release(k1)
    sincos(C, ang, SEQ, sinT, cosT)
    S.ts(sinT, sinT, prm["sgn"][:, 0:1], None, ALU.mult)
    A.release(k2)

    def rope_tile(pp, ps_, gcol, gscol, dst, tb, psb):
        sq, k1 = A.alloc([512], BF16)
        S.act(sq, pp, AF.Square)
        S.mm(psb, bones, sq)
        A.release(k1)
        rs, k2 = A.alloc([512], F32)
        C.rstd(rs, psb, 1.0 / 64)
        b, k4 = A.alloc([512], F32)
        S.stt(b, ps_, gscol, rs, ALU.mult, ALU.mult)
        S.stt(rs, pp, gcol, rs, ALU.mult, ALU.mult)
        S.tt(rs, rs, cosT[:, tb * 512:(tb + 1) * 512], ALU.mult)
        S.tt(b, b, sinT[:, tb * 512:(tb + 1) * 512], ALU.mult)
        S.tt(dst, rs, b, ALU.add)
        for k in (k2, k4):
            A.release(k)

    krT, _ = A.alloc([SEQ], BF16)
    wkr = P.load(w_in, 8, KR0, 256)
    for tb in range(4):
        pp, ps_ = bank(0), bank(1)
        proj_ps(pp, wkr, 8, 0, hT, tb)
        proj_ps(ps_, wkr, 8, 128, hT, tb)
        rope_tile(pp, ps_, prm["krg"][:, 0:1], prm["krg"][:, 1:2], krT[:, tb * 512:(tb + 1) * 512], tb, bank(2))

    P.free()
    P = C.WPool(1, 0, 64)
    wg_ring = [A.alloc([8, 128], BF16)[0] for _ in range(2)]
    wbuf = [{"wr": A.alloc([4, 128], BF16)[0], "wrs": A.alloc([4, 128], BF16)[0], "wqn": A.alloc([4, 128], BF16)[0],
             "wkn": A.alloc([2, 128], BF16)[0], "wv": A.alloc([2, 128], BF16)[0]} for _ in range(2)]
    wo_buf, _ = A.alloc([2, 1024], BF16)
    pairs, heads = {}, {}
    tcnt = [0]

    def tbank():
        tcnt[0] += 1
        return bank(3 + tcnt[0] % 2)
    pending = []

    def gate_mul4(dst, tb, val_f32, wg):
        pg = tbank()
        proj_ps(pg, wg, 8, 0, hT, tb)
        C.silu_gate(dst, val_f32, pg)

    def prep_units(h):
        hp, hh = divmod(h, 2)
        W = wbuf[h % 2]
        st = {}
        units = []

        def u_alloc():
            if hh == 0:
                qr, k_qr = A.alloc([SEQ], BF16)
                pairs[hp] = {"qr": qr, "k_qr": k_qr}
                P.load(w_uq, 4, 1536 + hp * 128, 128, dst=W["wr"])
                P.load(w_uq, 4, 2304 + hp * 128, 128, dst=W["wrs"])
            qn, k_qn = A.alloc([SEQ], BF16)
            kn, k_kn = A.alloc([SEQ], BF16)
            Vh, k_V = A.alloc([16, 128], BF16)
            P.load(w_uq, 4, h * 128, 128, dst=W["wqn"])
            P.load(w_uk, 2, h * 128, 128, dst=W["wkn"])
            P.load(w_uv, 2, h * 128, 128, dst=W["wv"])
            wg = P.load(w_in, 8, G0 + h * 128, 128, dst=wg_ring[h % 2])
            heads[h] = {"qn": qn, "kn": kn, "Vh": Vh, "wg": wg, "k_qn": k_qn, "k_kn": k_kn, "k_V": k_V}
        units.append(u_alloc)

        def r1(tb):
            pp, ps_ = bank(0), bank(1)
            proj_ps(pp, W["wr"], 4, 0, cqn, tb)
            proj_ps(ps_, W["wrs"], 4, 0, cqn, tb)
            sq, k1 = A.alloc([512], BF16)
            S.act(sq, pp, AF.Square)
            st["r", tb] = (sq, k1)

        def r2(tb):
            pp, ps_ = bank(0), bank(1)
            sq, k1 = st.pop(("r", tb))
            psb = tbank()
            S.mm(psb, bones, sq)
            A.release(k1)
            rs, k2 = A.alloc([512], F32)
            C.rstd(rs, psb, 1.0 / 64)
            b, k4 = A.alloc([512], F32)
            S.stt(b, ps_, prm["qrg"][:, 1:2], rs, ALU.mult, ALU.mult)
            S.stt(rs, pp, prm["qrg"][:, 0:1], rs, ALU.mult, ALU.mult)
            S.tt(rs, rs, cosT[:, tb * 512:(tb + 1) * 512], ALU.mult)
            S.tt(b, b, sinT[:, tb * 512:(tb + 1) * 512], ALU.mult)
            S.tt(pairs[hp]["qr"][:, tb * 512:(tb + 1) * 512], rs, b, ALU.add)
            A.release(k2)
            A.release(k4)

        def n1(kind, tb):
            pb = bank(tb % 2)
            if kind == "q":
                proj_ps(pb, W["wqn"], 4, 0, cqn, tb)
            else:
                proj_ps(pb, W["wkn"], 2, 0, ckvn, tb)
            sq, k1 = A.alloc([512], BF16)
            S.act(sq, pb, AF.Square)
            st[kind, tb] = (sq, k1)

        def n2(kind, tb):
            pb = bank(tb % 2)
            sq, k1 = st.pop((kind, tb))
            psb = tbank()
            S.mm(psb, C.ones, sq)
            A.release(k1)
            rs, k2 = A.alloc([512], F32)
            C.rstd(rs, psb, 1.0 / 128)
            dst = heads[h]["qn" if kind == "q" else "kn"][:, tb * 512:(tb + 1) * 512]
            S.stt(dst, pb, prm["qng" if kind == "q" else "kng"][:, 0:1], rs, ALU.mult, ALU.mult)
            A.release(k2)

        def vv(t4):
            pb = bank(t4 % 2)
            Vh = heads[h]["Vh"]
            for j in range(4):
                tt = 4 * t4 + j
                for k in range(2):
                    S.mm(pb[:, j * 128:(j + 1) * 128], ckvn[:, k, tt * 128:(tt + 1) * 128], W["wv"][:, k, :], start=(k == 0), stop=(k == 1))
            S.copy(Vh[:, 4 * t4:4 * t4 + 4, :], pb.rearrange("p (j d) -> p j d", j=4), eng="act")

        if hh == 0:
            for tb in range(4):
                units.append(lambda tb=tb: r1(tb))
                units.append(lambda tb=tb: r2(tb))
        for kind in ("q", "k"):
            for tb in range(4):
                units.append(lambda kind=kind, tb=tb: n1(kind, tb))
                units.append(lambda kind=kind, tb=tb: n2(kind, tb))
        for t4 in range(4):
            units.append(lambda t4=t4: vv(t4))
        return units

    def outproj_units(hp, oe, oo, toks):
        units = []

        def u_load():
            P.load(C.wout_d[l][2 * hp * 128:(2 * hp + 2) * 128, :], 2, 0, 1024, dst=wo_buf)
        units.append(u_load)

        def u(tt, dh):
            po = tbank()
            S.mm(po, oe[:, tt * 128:(tt + 1) * 128], wo_buf[:, 0, dh * 512:(dh + 1) * 512], start=True, stop=False)
            S.mm(po, oo[:, tt * 128:(tt + 1) * 128], wo_buf[:, 1, dh * 512:(dh + 1) * 512], start=False, stop=True)
            xs = C.x1[:, tt, dh * 512:(dh + 1) * 512]
            S.tt(xs, po, xs, ALU.add)
        for tt in range(16):
            for dh in range(2):
                units.append(lambda tt=tt, dh=dh: u(tt, dh))

        def u_rel():
            for k in toks:
                A.release(k)
        units.append(u_rel)
        return units

    def attn(h):
        hp, hh = divmod(h, 2)
        hb = 64 * hh
        H = heads[h]
        qn, kn, Vh, wg = H["qn"], H["kn"], H["Vh"], H["wg"]
        qr = pairs[hp]["qr"]
        items = []
        for qb in range(4):
            nkt = 4 * qb + 4
            for kt in range(nkt):
                items.append((qb, kt, nkt))
        pst_of = {}

        def stage_qk(idx):
            qb, kt, nkt = items[idx]
            j = kt - 4 * qb
            c0 = 128 * j if j > 0 else 0
            n = 512 - c0
            pst = bank(6 + idx % 2)
            q0 = qb * 512 + c0
            S.mm(pst[:, 0:n], kn[:, kt * 128:(kt + 1) * 128], qn[:, q0:q0 + n], start=True, stop=False)
            S.mm(pst[:, 0:n], krT[hb:hb + 64, kt * 128:(kt + 1) * 128], qr[hb:hb + 64, q0:q0 + n], start=False, stop=True)
            pT, k2 = A.alloc([512], BF16)
            S.act(pT[:, 0:n], pst[:, 0:n], AF.Exp, scale=scale)
            if j >= 0:
                S.tt(pT[:, 0:128], pT[:, 0:128], tri, ALU.mult)
            pst_of[idx] = (pT, k2, c0, n)

        def stage_pv(idx):
            qb, kt, nkt = items[idx]
            pT, k2, c0, n = pst_of.pop(idx)
            pacc, psum_ = bank(2), bank(5)
            S.mm(pacc[:, c0:512], Vh[:, kt, :], pT[:, 0:n], start=(kt == 0), stop=(kt == nkt - 1))
            S.mm(psum_[:, c0:512], C.ones, pT[:, 0:n], start=(kt == 0), stop=(kt == nkt - 1))
            A.release(k2)
            if kt == nkt - 1:
                rinv, k3 = A.alloc([512], F32)
                mo, k4 = A.alloc([512], F32)
                S.act(rinv, psum_, AF.Ln)
                S.copy(mo, pacc, eng="act")
                S.act(rinv, rinv, AF.Exp, scale=-1.0)
                S.tt(mo, mo, rinv, ALU.mult)
                gate_mul4(qn[:, qb * 512:(qb + 1) * 512], qb, mo, wg)
                A.release(k3)
                A.release(k4)

        npop = -(-len(pending) // (len(items) - 2))
        stage_qk(0)
        for idx in range(len(items)):
            if idx + 1 < len(items):
                stage_qk(idx + 1)
            stage_pv(idx)
            for _ in range(npop):
                if pending:
                    pending.pop(0)()
        while pending:
            pending.pop(0)()
        A.release(H["k_kn"])
        A.release(H["k_V"])
        if hh == 1:
            He = heads[h - 1]
            A.release(pairs[hp]["k_qr"])
            pending.extend(outproj_units(hp, He["qn"], qn, (He["k_qn"], H["k_qn"])))

    for u_ in prep_units(0):
        u_()
    for h in range(12):
        if h + 1 < 12:
            pending.extend(prep_units(h + 1))
        attn(h)
    while pending:
        pending.pop(0)()
    P.free()
    for k in list(A.live):
        if k not in snap0:
            A.release(k)


def pack_shared(inp):
    f = np.float32
    sh = {}
    sh["c_ident"] = np.eye(128, dtype=f)
    lane = np.arange(128)
    sh["c_maskL"] = np.stack([(lane // 64 == 0), (lane // 64 == 1)], 1).astype(f)
    sh["c_maskLn"] = -sh["c_maskL"]
    g2r = (lane // 16) % 2
    sh["c_maskR"] = np.stack([(g2r == 0), (g2r == 1)], 1).astype(f)
    sv = np.concatenate([np.arange(8), -np.arange(8)]).astype(f)
    sh["c_sv"] = np.broadcast_to(sv, (128, 16)).copy()
    sh["c_jiota"] = np.broadcast_to(np.arange(256, dtype=f), (128, 256)).copy()
    sh["ln_gain"] = inp["ln_gain"]
    sh["mem_norm"] = inp["mem_norm"]
    sh["p_xq_norm"] = np.ascontiguousarray(inp["xq_norm"].T)
    sh["p_xk_norm"] = np.ascontiguousarray(inp["xk_norm"].T)
    sh["w_out"] = inp["w_out"]
    sh["w_mem_kv"] = inp["w_mem_kv"]
    sh["s5_w_in"] = inp["s5_w_in"][0]
    sh["s5_w_glu"] = inp["s5_w_glu"][0]
    lamr, lami, lst = inp["s5_lambda_re"][0], inp["s5_lambda_im"][0], inp["s5_log_step"][0]
    toL = lambda a: np.ascontiguousarray(a.reshape(48, 2, 64).transpose(1, 2, 0).reshape(128, 48))
    sh["s5_lamr_L"] = toL(lamr)
    sh["s5_lami_L"] = toL(lami)
    sh["s5_lstep_L"] = toL(np.broadcast_to(lst[:, None], (96, 64)))
    toL3 = lambda a: np.ascontiguousarray(a.reshape(48, 2, 64, 16).transpose(1, 2, 0, 3).reshape(128, 48, 16))
    sh["s5_B_r_L"] = toL3(inp["s5_b_re"][0])
    sh["s5_B_i_L"] = toL3(inp["s5_b_im"][0])
    sh["s5_C_r_L"] = toL3(inp["s5_c_re"][0].transpose(0, 2, 1))
    sh["s5_C_i_L"] = toL3(inp["s5_c_im"][0].transpose(0, 2, 1))
    def toR(a):
        return np.ascontiguousarray(a.reshape(12, 4, 2, 64, 16).transpose(1, 2, 4, 0, 3).reshape(128, 12, 64))
    rep = lambda a: np.broadcast_to(a[:, :, None], (96, 64, 16))
    sh["s5_lamr_R"] = toR(rep(lamr))
    sh["s5_lami_R"] = toR(rep(lami))
    sh["s5_lstep_R"] = toR(rep(np.broadcast_to(lst[:, None], (96, 64))))
    sh["s5_B_r_R"] = toR(inp["s5_b_re"][0])
    sh["s5_B_i_R"] = toR(inp["s5_b_im"][0])
    sh["s5_d_col"] = np.ascontiguousarray(inp["s5_d"][0].reshape(12, 128).T)
    wi = inp["mla_w_in"][0]
    kr = wi[:, 768:832]
    krs = np.concatenate([kr[:, 32:], kr[:, :32]], 1)
    sh["mla_w_in_ext"] = np.concatenate([wi[:, 0:768], wi[:, 832:3392], kr, kr, krs, krs], 1)
    wq = inp["mla_w_uq"][0].reshape(512, 12, 192)
    nope = wq[:, :, :128].reshape(512, 1536)
    ropep = wq[:, :, 128:]
    ropes = np.concatenate([ropep[:, :, 32:], ropep[:, :, :32]], 2)
    sh["mla_w_uq_ext"] = np.concatenate([nope, ropep.reshape(512, 768), ropes.reshape(512, 768)], 1)
    wkv = inp["mla_w_ukv"][0].reshape(256, 12, 256)
    sh["mla_w_ukv_k"] = wkv[:, :, :128].reshape(256, 1536)
    sh["mla_w_ukv_v"] = wkv[:, :, 128:].reshape(256, 1536)
    sh["mla_qlg"] = inp["mla_q_lora_norm"][0].reshape(4, 128).T
    sh["mla_kvlg"] = inp["mla_kv_lora_norm"][0].reshape(2, 128).T
    sh["mla_qng"] = inp["mla_q_nope_norm"][0].reshape(128, 1)
    sh["mla_kng"] = inp["mla_k_nope_norm"][0].reshape(128, 1)
    def ropeg(gv):
        gs = np.concatenate([gv[32:], gv[:32]])
        return np.stack([np.concatenate([gv, gv]), np.concatenate([gs, gs])], 1)
    sh["mla_qrg"] = ropeg(inp["mla_q_rope_norm"][0])
    sh["mla_krg"] = ropeg(inp["mla_k_rope_norm"][0])
    half = 32
    invf = (10000.0 ** (-np.arange(half, dtype=np.float32) / half)).astype(f)
    p64 = lane % 64
    sh["mla_invf"] = invf[p64 % 32].reshape(128, 1)
    sh["mla_sgn"] = np.where(p64 < 32, -1.0, 1.0).astype(f).reshape(128, 1)
    kk = np.arange(128)
    sh["mla_tri"] = (kk[:, None] <= kk[None, :]).astype(f)
    sh["mla_bones"] = (kk[:, None] // 64 == kk[None, :] // 64).astype(f)
    return {k: np.ascontiguousarray(v, dtype=v.dtype) for k, v in sh.items()}


_NC_CACHE = {}


def kernel(**inputs):
    inp = {k: np.asarray(v) for k, v in inputs.items()}
    sh = pack_shared(inp)
    if "nc" not in _NC_CACHE:
        _NC_CACHE["nc"] = build()
    nc = _NC_CACHE["nc"]
    in_maps = []
    for b in range(8):
        m = dict(sh)
        m["x"] = np.ascontiguousarray(inp["x"][b], dtype=np.float32)
        m["mem"] = np.ascontiguousarray(inp["mem"][b], dtype=np.float32)
        m["pos"] = np.ascontiguousarray(inp["positions"][b:b + 1], dtype=np.int32)
        in_maps.append(m)
    res = run_bass_kernel_spmd(nc, in_maps, core_ids=list(range(8)))
    return np.stack([np.asarray(r["out"], dtype=np.float32) for r in res.results], 0)
```
